# Optimizing a Trainium2 kernel written in Bass

```python
import math
import jax, jax.numpy as jnp
from jax import lax
import numpy as np

D_MODEL = 1024
BATCH = 4
SEQ = 4096
DEPTH = 1
DEC_BATCH = 16
DEC_SEQ = 32
PAST_LEN = 4096

CHUNK = 64
N_META = 16
HEAD_DIM = 64
SB_HEADS = 8
FOX_HEADS = 8
SB_WIDTH = SB_HEADS * HEAD_DIM
FOX_WIDTH = FOX_HEADS * HEAD_DIM
MIX_WIDTH = SB_WIDTH + FOX_WIDTH
IN_WIDTH = 4 * SB_WIDTH + 4 * FOX_WIDTH + FOX_HEADS
Q_BLOCK = 128
EPS = 1e-6

kernel_name = 'hymba_stickbreak_fox_stream_step'


def rmsnorm(x, g):
    xf = x.astype(jnp.float32)
    y = xf * lax.rsqrt(jnp.mean(xf * xf, axis=-1, keepdims=True) + EPS)
    return (y * g.astype(jnp.float32)).astype(x.dtype)


def project(h, g_norm, w_in, b_f):
    B, T, _ = h.shape
    u = rmsnorm(h, g_norm) @ w_in
    s, f = SB_WIDTH, FOX_WIDTH
    qa, ka, va, ga, qb, kb, vb, gb, fl = jnp.split(
        u, [s, 2 * s, 3 * s, 4 * s, 4 * s + f, 4 * s + 2 * f, 4 * s + 3 * f, 4 * s + 4 * f], axis=-1)
    heads = lambda t, n: t.reshape(B, T, n, HEAD_DIM)
    logf = jax.nn.log_sigmoid(fl.astype(jnp.float32) + b_f.astype(jnp.float32))
    return (heads(qa, SB_HEADS), heads(ka, SB_HEADS), heads(va, SB_HEADS), ga,
            heads(qb, FOX_HEADS), heads(kb, FOX_HEADS), heads(vb, FOX_HEADS), gb, logf)


def combine(h, oa, ga, ob, gb, w_out):
    B, T, _ = h.shape
    mixed = jnp.concatenate([oa.reshape(B, T, SB_WIDTH) * jax.nn.silu(ga),
                             ob.reshape(B, T, FOX_WIDTH) * jax.nn.silu(gb)], axis=-1)
    return h + mixed @ w_out


def sb_block(q, k, v, q_start):
    Tq, Tk = q.shape[1], k.shape[1]
    z = jnp.einsum('bqhd,bkhd->bhqk', q, k).astype(jnp.float32) * (1.0 / math.sqrt(HEAD_DIM))
    t_idx = q_start + jnp.arange(Tq)[:, None]
    s_idx = jnp.arange(Tk)[None, :]
    before = s_idx < t_idx
    log_keep = jnp.where(before, jax.nn.log_sigmoid(-z), 0.0)
    later = lax.cumsum(log_keep, axis=3, reverse=True) - log_keep
    w = jnp.where(before, jnp.exp(jax.nn.log_sigmoid(z) + later), 0.0)
    return jnp.einsum('bhqk,bkhd->bqhd', w.astype(v.dtype), v)


def fox_block(q, k, v, c, q_start):
    Tq, Tk = q.shape[1], k.shape[1]
    logits = jnp.einsum('bqhd,bkhd->bhqk', q, k).astype(jnp.float32) * (1.0 / math.sqrt(HEAD_DIM))
    c_q = jnp.transpose(c[:, q_start:q_start + Tq], (0, 2, 1))
    c_k = jnp.transpose(c, (0, 2, 1))
    decay = c_q[:, :, :, None] - c_k[:, :, None, :]
    causal = jnp.arange(Tk)[None, :] <= (q_start + jnp.arange(Tq)[:, None])
    p = jax.nn.softmax(jnp.where(causal, logits + decay, -jnp.inf), axis=-1)
    return jnp.einsum('bhqk,bkhd->bqhd', p.astype(v.dtype), v)


def attend_prompt(qa, ka, va, qb, kb, vb, logf):
    L = qa.shape[1]
    c = jnp.cumsum(logf, axis=1)
    oa, ob = [], []
    for start in range(0, L, Q_BLOCK):
        end = min(start + Q_BLOCK, L)
        oa.append(sb_block(qa[:, start:end], ka[:, :end], va[:, :end], start))
        ob.append(fox_block(qb[:, start:end], kb[:, :end], vb[:, :end], c[:, :end], start))
    return jnp.concatenate(oa, axis=1), jnp.concatenate(ob, axis=1)


def attend_sample(qa, ka, va, qb, kb, vb, logf, ck_a, cv_a, ck_b, cv_b, clogf):
    P = ck_a.shape[1]
    ka_all = jnp.concatenate([ck_a.astype(ka.dtype), ka], axis=1)
    va_all = jnp.concatenate([cv_a.astype(va.dtype), va], axis=1)
    kb_all = jnp.concatenate([ck_b.astype(kb.dtype), kb], axis=1)
    vb_all = jnp.concatenate([cv_b.astype(vb.dtype), vb], axis=1)
    c = jnp.cumsum(jnp.concatenate([clogf.astype(jnp.float32), logf], axis=1), axis=1)
    return sb_block(qa, ka_all, va_all, P), fox_block(qb, kb_all, vb_all, c, P)


def setup_inputs(seed: int = 0) -> dict:
    key = jax.random.key(seed)
    ks = jax.random.split(key, 14)
    nrm = jax.random.normal
    return {
        'x_prompt': nrm(ks[0], (BATCH, SEQ, D_MODEL), jnp.float32),
        'x_sample': nrm(ks[1], (DEC_BATCH, DEC_SEQ, D_MODEL), jnp.float32),
        'cache_a_k': nrm(ks[2], (DEPTH, DEC_BATCH, PAST_LEN, SB_HEADS, HEAD_DIM), jnp.float32),
        'cache_a_v': nrm(ks[3], (DEPTH, DEC_BATCH, PAST_LEN, SB_HEADS, HEAD_DIM), jnp.float32),
        'cache_b_k': nrm(ks[4], (DEPTH, DEC_BATCH, PAST_LEN, FOX_HEADS, HEAD_DIM), jnp.float32),
        'cache_b_v': nrm(ks[5], (DEPTH, DEC_BATCH, PAST_LEN, FOX_HEADS, HEAD_DIM), jnp.float32),
        'cache_b_logf': jax.nn.log_sigmoid(3.0 + nrm(ks[6], (DEPTH, DEC_BATCH, PAST_LEN, FOX_HEADS), jnp.float32)),
        'meta_tokens': nrm(ks[7], (N_META, D_MODEL), jnp.float32),
        'norm_g': 1.0 + 0.1 * nrm(ks[8], (DEPTH, D_MODEL), jnp.float32),
        'w_in': nrm(ks[9], (DEPTH, D_MODEL, IN_WIDTH), jnp.float32) * D_MODEL ** -0.5,
        'b_f': 2.0 + 2.0 * jax.random.uniform(ks[10], (DEPTH, FOX_HEADS), jnp.float32),
        'w_out': nrm(ks[11], (DEPTH, MIX_WIDTH, D_MODEL), jnp.float32) * MIX_WIDTH ** -0.5,
        'final_g': 1.0 + 0.1 * nrm(ks[12], (D_MODEL,), jnp.float32),
    }


def reference(x_prompt, x_sample, cache_a_k, cache_a_v, cache_b_k, cache_b_v, cache_b_logf,
              meta_tokens, norm_g, w_in, b_f, w_out, final_g):
    B = x_prompt.shape[0]
    meta = jnp.broadcast_to(meta_tokens[None].astype(x_prompt.dtype), (B, N_META, D_MODEL))
    hp = jnp.concatenate([meta, x_prompt], axis=1)
    hs = x_sample
    pak, pav, pbk, pbv, pbl = [], [], [], [], []
    sak, sav, sbk, sbv, sbl = [], [], [], [], []
    for l in range(DEPTH):
        qa, ka, va, ga, qb, kb, vb, gb, lf = project(hp, norm_g[l], w_in[l], b_f[l])
        oa, ob = attend_prompt(qa, ka, va, qb, kb, vb, lf)
        hp = combine(hp, oa, ga, ob, gb, w_out[l])
        pak.append(ka); pav.append(va); pbk.append(kb); pbv.append(vb); pbl.append(lf)
        qa, ka, va, ga, qb, kb, vb, gb, lf = project(hs, norm_g[l], w_in[l], b_f[l])
        oa, ob = attend_sample(qa, ka, va, qb, kb, vb, lf, cache_a_k[l], cache_a_v[l],
                               cache_b_k[l], cache_b_v[l], cache_b_logf[l])
        hs = combine(hs, oa, ga, ob, gb, w_out[l])
        sak.append(ka); sav.append(va); sbk.append(kb); sbv.append(vb); sbl.append(lf)
    y_prompt = rmsnorm(hp, final_g)[:, N_META:]
    y_sample = rmsnorm(hs, final_g)
    return (y_prompt, y_sample,
            jnp.stack(pak), jnp.stack(pav), jnp.stack(pbk), jnp.stack(pbv), jnp.stack(pbl),
            jnp.stack(sak), jnp.stack(sav), jnp.stack(sbk), jnp.stack(sbv), jnp.stack(sbl))
```

```python
import os
import numpy as np
from contextlib import ExitStack
import concourse.bass as bass
import concourse.mybir as mybir
from concourse.bass_utils import run_bass_kernel_spmd

F32 = mybir.dt.float32
BF16 = mybir.dt.bfloat16
AF = mybir.ActivationFunctionType
ALU = mybir.AluOpType
NEGV = -30000.0
EPS = 1e-6

C_ID, C_TRI, C_ONE, C_TLE, C_O192, C_NS, C_NH = 0, 128, 256, 384, 512, 704, 768
CW = 769


class _Stop(Exception):
    pass


def _stage(n):
    if float(os.environ.get('KSTAGE', '99')) <= n:
        raise _Stop()


class Sched:
    def __init__(self, nc, es):
        self.nc, self.es = nc, es
        self.q = {k: [] for k in ("pe", "act", "dve", "pool", "sp")}
        self.sem, self.cnt = {}, {}
        self.seen = {k: {} for k in self.q}
        self.st = {}
        for k in self.q:
            self._mk(k)

    def _mk(self, key):
        self.sem[key] = self.es.enter_context(self.nc.semaphore("s_" + key))
        self.cnt[key] = 0

    def _deps(self, e, reads, writes):
        deps = []
        for b in reads:
            s = self.st.get(b)
            if s and s["w"]:
                deps.append(s["w"])
        for b in writes:
            s = self.st.get(b)
            if s:
                if s["w"]:
                    deps.append(s["w"])
                deps.extend(s["r"].items())
        waits = []
        for (k, v) in deps:
            if k == e and (e == "pe" or v > self.cnt[e]):
                continue
            if e == "sp" and k not in self.q:
                v = self.cnt[k]
            if self.seen[e].get(k, 0) < v:
                self.seen[e][k] = v
                waits.append((k, v))
        return waits

    def _mark(self, key, val, reads, writes):
        for b in reads:
            s = self.st.setdefault(b, {"w": None, "r": {}})
            s["r"][key] = max(s["r"].get(key, 0), val)
        for b in writes:
            self.st[b] = {"w": (key, val), "r": {}}

    def op(self, e, fn, reads=(), writes=(), inc=True, selfwait=False):
        waits = self._deps(e, reads, writes)
        if selfwait and self.cnt[e] > self.seen[e].get(e, 0):
            self.seen[e][e] = self.cnt[e]
            waits.append((e, self.cnt[e]))
        val = self.cnt[e] + 1
        if inc:
            self.cnt[e] = val
        self.q[e].append((waits, fn, (e, 1) if inc else None))
        self._mark(e, val, reads, writes)

    def dma(self, qe, out, in_, reads=(), writes=(), key=None):
        if key not in self.sem:
            self._mk(key)
        waits = self._deps(qe, reads, writes)
        self.cnt[key] += 16
        eng = {"sp": self.nc.sync, "pool": self.nc.gpsimd, "act": self.nc.scalar}[qe]
        self.q[qe].append((waits, lambda: eng.dma_start(out=out, in_=in_), (key, 16)))
        self._mark(key, self.cnt[key], reads, writes)

    def emit(self):
        nc = self.nc
        fin = [(k, v) for k, v in self.cnt.items() if k not in self.q and v > 0]
        self.q["sp"].append((fin, None, None))
        for e in ("act",):
            self.q[e].append(([(k, self.cnt[k]) for k in ("pe", "dve", "pool") if self.cnt[k] > 0], None, None))

        def mk(k):
            def body(eng):
                for waits, fn, inc in self.q[k]:
                    for (sk, v) in waits:
                        eng.wait_ge(self.sem[sk], v)
                    if fn is None:
                        continue
                    ins = fn()
                    if inc:
                        ins.then_inc(self.sem[inc[0]], inc[1])
            return body

        with nc.Block() as block:
            block.tensor(mk("pe"))
            block.scalar(mk("act"))
            block.vector(mk("dve"))
            block.gpsimd(mk("pool"))
            block.sync(mk("sp"))


def build():
    nc = bass.Bass("TRN2", target_bir_lowering=False)
    es = ExitStack()
    S = Sched(nc, es)

    def din(n, shp, dt=F32):
        return nc.dram_tensor(n, list(shp), dt, kind="ExternalInput").ap()

    def dout(n, shp, dt=F32):
        return nc.dram_tensor(n, list(shp), dt, kind="ExternalOutput").ap()

    def sb(n, shp, dt=F32):
        return es.enter_context(nc.sbuf_tensor(n, list(shp), dt))

    def ps(n, shp, dt=F32):
        return es.enter_context(nc.psum_tensor(n, list(shp), dt))

    xall = din("xall", [4096, 1024]); meta = din("meta", [16, 1024]); xown = din("xown", [2048, 1024])
    xs = din("xs", [64, 1024])
    w_in = din("w_in", [1024, 4104]); w_out = din("w_out", [1024, 1024])
    ng = din("ng", [128, 8]); fg = din("fg", [128, 1024]); bfb = din("bfb", [128, 8])
    cach = {("k", 0): din("cak", [2, 4096, 512]), ("v", 0): din("cav", [2, 4096, 512]),
            ("k", 1): din("cbk", [2, 4096, 512]), ("v", 1): din("cbv", [2, 4096, 512])}
    clf = din("clf", [2, 4096, 8])
    cst = din("cst", [128, CW]); cst2 = din("cst2", [128, 16]); negm = din("negm", [16, 128, 512])

    y_own = dout("y_own", [2048, 1024]); ys = dout("ys", [64, 1024])
    pko = {("k", 0): dout("pka", [4112, 512]), ("v", 0): dout("pva", [4112, 512]),
           ("k", 1): dout("pkb", [4112, 512]), ("v", 1): dout("pvb", [4112, 512])}
    plf = dout("plf", [4112, 8])
    sko = {("k", 0): dout("sak", [64, 512]), ("v", 0): dout("sav", [64, 512]),
           ("k", 1): dout("sbk", [64, 512]), ("v", 1): dout("sbv", [64, 512])}
    slf = dout("slf", [64, 8])
    mixT = nc.dram_tensor("mixT", [1024, 2048], BF16).ap()

    cstf = sb("cstf", [128, CW]); c2 = sb("c2", [128, 16])
    ident = sb("ident", [128, 128], BF16); tri = sb("tri", [128, 128], BF16)
    ones = sb("ones", [128, 128], BF16); o192 = sb("o192", [128, 192], BF16)
    negs = sb("negs", [128, 64], BF16)
    negs8 = sb("negs8", [32, 256], BF16)
    zr = sb("zr", [1, 256], BF16)
    ngt = sb("ngt", [128, 8]); fgt = sb("fgt", [128, 1024]); bft = sb("bft", [128, 8])
    negb = sb("negb", [128, 8, 512], BF16)
    gen = [sb("gen%d" % i, [128, 1032]) for i in range(2)]
    Wbf = sb("Wbf", [128, 8, 1032], BF16)
    xin = [sb("xin%d" % i, [128, 1024]) for i in range(2)]
    xnb = sb("xnb", [128, 1024], BF16)
    ssq = sb("ssq", [128, 1]); rs = sb("rs", [128, 1]); rs2 = sb("rs2", [128, 1])
    xnT = sb("xnT", [128, 8, 512], BF16)
    xnTs = sb("xnTs", [128, 8, 64], BF16)
    KT = sb("KT", [128, 4, 4128], BF16)
    Vp = sb("Vp", [128, 33, 4, 192], BF16)
    QT = sb("QT", [128, 4, 512], BF16); SG = sb("SG", [128, 4, 512], BF16)
    KTs = sb("KTs", [128, 4, 64], BF16); QTs = sb("QTs", [128, 4, 64], BF16); SGs = sb("SGs", [128, 4, 64], BF16)
    smix = sb("smix", [128, 8, 64], BF16)
    stg = [sb("stg%d" % i, [128, 512]) for i in range(3)]
    et = [sb("et%d" % i, [128, 512]) for i in range(3)]
    spt = [sb("spt%d" % i, [128, 512], BF16) for i in range(2)]
    Ssum = sb("Ssum", [128, 512], BF16)
    Et = sb("Et", [128, 512])
    wt = [sb("wt%d" % i, [128, 512], BF16) for i in range(2)]
    Pacc, Pbf, rcp = et[0], spt[0], Et
    mixb = [sb("mixb%d" % i, [128, 512], BF16) for i in range(2)]
    ckb = mixb
    svt = [sb("svt%d" % s_, [32, 4, 192], BF16) for s_ in range(2)]
    slt = [sb("slt%d" % s_, [32, 8]) for s_ in range(2)]
    lfa = sb("lfa", [128, 33, 8]); totS = sb("totS", [128, 33, 8]); pre = sb("pre", [128, 34, 8])
    negc = sb("negc", [128, 33, 8]); crefO = sb("crefO", [128, 16, 8]); maskc = sb("maskc", [128, 4, 8])
    biasq = [sb("biasq%d" % i, [128, 4, 33]) for i in range(2)]
    cS = sb("cS", [128, 33, 8]); r1 = sb("r1", [128, 33, 8])
    comp = [sb("comp%d" % i, [128, 33, 8], BF16) for i in range(3)]
    CA = sb("CA", [128, 33, 64], BF16); CB = sb("CB", [128, 1, 64], BF16); t1 = sb("t1", [128, 16, 8])
    Aall = sb("Aall", [64, 4128], BF16); Ball = sb("Ball", [64, 64], BF16)
    Bh = [sb("Bh%d" % i, [64, 512], BF16) for i in range(2)]
    lft = sb("lft", [128, 8]); lft2 = sb("lft2", [128, 8])

    pT = ps("pT", [128, 1024], BF16)
    pp = [ps("pp%d" % i, [128, 512]) for i in range(2)]
    zbs = [ps("zb%d" % i, [128, 512]) for i in range(2)]
    gb = ps("gb", [128, 512]); ob = ps("ob", [128, 512]); rb = ps("rb", [128, 512])

    V, SC, G, PE = nc.vector, nc.scalar, nc.gpsimd, nc.tensor
    ctr = {"ppx": 0, "sp": 0, "gb": 0, "pp": 0, "zb": 0, "stg": 0, "xin": 0, "gen": 0, "e": 0, "w": 0, "mix": 0, "bh": 0, "ckb": 0, "mT": 0}

    def rot(name, n):
        i = ctr[name] % n
        ctr[name] += 1
        return i

    S.dma("sp", cstf[:, :], cst, writes=["cstf"], key="c0")
    S.dma("sp", c2[:, :], cst2, writes=["c2"], key="c1")
    S.dma("sp", ngt[:, :], ng, writes=["ngt"], key="c2")
    S.dma("sp", fgt[:, :], fg, writes=["fgt"], key="c3")
    S.dma("sp", bft[:, :], bfb, writes=["bft"], key="c4")
    for dst, off, wd, nm in ((ident, C_ID, 128, "ident"), (tri, C_TRI, 128, "tri"), (ones, C_ONE, 128, "ones"),
                             (o192, C_O192, 192, "o192"), (negs, C_NS, 64, "negs")):
        S.op("dve", lambda dst=dst, off=off, wd=wd: V.tensor_copy(out=dst[:, :], in_=cstf[:, off:off + wd]),
             reads=["cstf"], writes=[nm])
    tle_f = cstf[:, C_TLE:C_TLE + 128]
    one_f = cstf[:, C_ONE:C_ONE + 128]
    nh_f = cstf[:, C_NH:C_NH + 1]
    S.op("pool", lambda: G.memset(Vp[:, :, :, 64:128], 0.0), writes=["Vp0"])
    S.op("pool", lambda: G.memset(lfa[:, :, :], 0.0), writes=["lfa"])
    S.op("pool", lambda: G.memset(zr[:, :], 0.0), writes=["zr"])

    def load_w(colspecs, src=w_in, fold=True):
        for k in range(8):
            gi = rot("gen", 2)
            o = 0
            for (c0, wd) in colspecs:
                S.dma("sp", gen[gi][:, o:o + wd], src[k * 128:(k + 1) * 128, c0:c0 + wd],
                      writes=["gen%d" % gi], key="ldw%d" % gi)
                o += wd
            if fold:
                S.op("dve", lambda gi=gi, k=k, o=o: V.tensor_scalar(out=Wbf[:, k, 0:o], in0=gen[gi][:, 0:o],
                                                                   scalar1=ngt[:, k:k + 1], scalar2=None, op0=ALU.mult),
                     reads=["gen%d" % gi, "ngt"], writes=["Wbf"])
            else:
                S.op("dve", lambda gi=gi, k=k, o=o: V.tensor_copy(out=Wbf[:, k, 0:o], in_=gen[gi][:, 0:o]),
                     reads=["gen%d" % gi], writes=["Wbf"])

    def rstd_of(src_t, n, key):
        S.op("act", lambda: SC.activation(out=xnb[0:n, :], in_=src_t[0:n, :], func=AF.Square, accum_out=ssq[0:n, 0:1]),
             reads=[key], writes=["xnb", "ssq"])
        S.op("dve", lambda: V.tensor_scalar(out=rs[0:n, :], in0=ssq[0:n, :], scalar1=1.0 / 1024.0, scalar2=EPS,
                                            op0=ALU.mult, op1=ALU.add), reads=["ssq"], writes=["rs"])
        S.op("pool", lambda: G.tensor_tensor(out=rs2[0:n, :], in0=rs[0:n, :], in1=nh_f[0:n, :], op=ALU.pow),
             reads=["rs", "cstf"], writes=["rs2"])

    def norm_T(src, n, dstT, dkey):
        xi_i = rot("xin", 2)
        xi = xin[xi_i]
        xk = "xin%d" % xi_i
        S.dma("sp", xi[0:n, :], src, writes=[xk], key="ldx%d" % xi_i)
        rstd_of(xi, n, xk)
        S.op("dve", lambda: V.tensor_scalar(out=xnb[0:n, :], in0=xi[0:n, :], scalar1=rs2[0:n, 0:1], scalar2=None,
                                            op0=ALU.mult), reads=[xk, "rs2"], writes=["xnb"])
        for k in range(8):
            S.op("pe", lambda k=k: PE.transpose(out=pT[:, k * 128:k * 128 + n], in_=xnb[0:n, k * 128:(k + 1) * 128],
                                                identity=ident[0:n, 0:n]),
                 reads=["xnb", "ident"], writes=["pT"], inc=(k == 7))
        S.op("dve", lambda: V.tensor_copy(out=dstT, in_=pT[:, :].rearrange("p (k t) -> p k t", k=8)[:, :, 0:n]),
             reads=["pT"], writes=[dkey])

    ppx = [(pp[0], "pp0"), (pp[1], "pp1"), (zbs[0], "zb0"), (zbs[1], "zb1")]

    def next_pp():
        return ppx[rot("ppx", 4)]

    def proj_fm(wc0, xT, n, xkey):
        bank, bkey = next_pp()
        for k in range(8):
            S.op("pe", lambda k=k: PE.matmul(bank[:, 0:n], lhsT=Wbf[:, k, wc0:wc0 + 128], rhs=xT[:, k, 0:n],
                                             start=(k == 0), stop=(k == 7)),
                 reads=["Wbf", xkey], writes=[bkey], inc=(k == 7))
        return bank, bkey

    def proj_tm(wc0, wd, xT, t0, n, xkey):
        bank, bkey = next_pp()
        for k in range(8):
            S.op("pe", lambda k=k: PE.matmul(bank[0:n, 0:wd], lhsT=xT[:, k, t0:t0 + n], rhs=Wbf[:, k, wc0:wc0 + wd],
                                             start=(k == 0), stop=(k == 7)),
                 reads=["Wbf", xkey], writes=[bkey], inc=(k == 7))
        return bank, bkey

    def tm_kv_out(xT, t0, n, xkey, g, vblk, rows_out, outs):
        for wi, which in enumerate(("k", "v")):
            p_, pk_ = proj_tm(wi * 512, 512, xT, t0, n, xkey)
            si = rot("stg", 3)
            sk = "stg%d" % si
            if wi == 0:
                S.op("act", lambda p_=p_, si=si: SC.copy(out=stg[si][0:n, :], in_=p_[0:n, :]), reads=[pk_], writes=[sk])
            else:
                S.op("dve", lambda p_=p_, si=si: V.tensor_copy(out=stg[si][0:n, :], in_=p_[0:n, :]), reads=[pk_], writes=[sk])
            S.dma("sp", outs[(which, g)][rows_out:rows_out + n, :], stg[si][0:n, :], reads=[sk], key="st%d" % si)
            if which == "v" and vblk is not None:
                sv = stg[si][0:n, :].rearrange("t (p two d) -> t p two d", two=2, d=64)
                S.op("pool", lambda sv=sv: G.tensor_copy(out=Vp[0:n, vblk, :, 0:64], in_=sv[:, :, 0, :]),
                     reads=[sk], writes=["Vp"], inc=False)
                S.op("pool", lambda sv=sv: G.tensor_copy(out=Vp[0:n, vblk, :, 128:192], in_=sv[:, :, 1, :]),
                     reads=[sk], writes=["Vp"])

    def logf_out(xT, t0, n, xkey, dst_lf, rows_out, out_ap, dkey="lfa"):
        p_, pk_ = proj_tm(1024, 8, xT, t0, n, xkey)
        S.op("dve", lambda: V.tensor_tensor(out=lft[0:n, :], in0=p_[0:n, 0:8], in1=bft[0:n, :], op=ALU.add),
             reads=[pk_, "bft"], writes=["lft"])
        S.op("act", lambda: SC.activation(out=lft2[0:n, :], in_=lft[0:n, :], func=AF.Exp, scale=-1.0),
             reads=["lft"], writes=["lft2"])
        S.op("act", lambda: SC.activation(out=lft[0:n, :], in_=lft2[0:n, :], func=AF.Ln, bias=1.0),
             reads=["lft2"], writes=["lft"])
        S.op("dve", lambda: V.tensor_scalar(out=dst_lf, in0=lft[0:n, :], scalar1=-1.0, scalar2=None, op0=ALU.mult),
             reads=["lft"], writes=[dkey])
        S.dma("sp", out_ap[rows_out:rows_out + n, :], dst_lf, reads=[dkey], key="stlf")

    def decay_prep(nb, blkspec):
        lf2 = lfa[:, 0:nb, :].rearrange("p b h -> p (b h)")
        S.op("pe", lambda: PE.matmul(pp[0][:, 0:nb * 8], lhsT=tle_f, rhs=lf2, start=True, stop=True),
             reads=["lfa", "cstf"], writes=["pp0"])
        S.op("pe", lambda: PE.matmul(pp[1][:, 0:nb * 8], lhsT=one_f, rhs=lf2, start=True, stop=True),
             reads=["lfa", "cstf"], writes=["pp1"])
        S.op("dve", lambda: V.tensor_copy(out=totS[:, 0:nb, :], in_=pp[1][:, 0:nb * 8].rearrange("p (b h) -> p b h", h=8)),
             reads=["pp1"], writes=["totS"])
        S.op("dve", lambda: V.memset(pre[:, 0, :], 0.0), writes=["pre"])
        for b in range(1, nb + 1):
            S.op("dve", lambda b=b: V.tensor_tensor(out=pre[:, b, :], in0=pre[:, b - 1, :], in1=totS[:, b - 1, :], op=ALU.add),
                 reads=["pre", "totS"], writes=["pre"])
        S.op("dve", lambda: V.tensor_tensor(out=cS[:, 0:nb, :], in0=pp[0][:, 0:nb * 8].rearrange("p (b h) -> p b h", h=8),
                                            in1=pre[:, 0:nb, :], op=ALU.add), reads=["pp0", "pre"], writes=["cS"])
        S.op("dve", lambda: V.tensor_scalar(out=negc[:, 0:nb, :], in0=cS[:, 0:nb, :], scalar1=-1.0, scalar2=None, op0=ALU.mult),
             reads=["cS"], writes=["negc"])
        S.op("dve", lambda: V.tensor_copy(out=comp[0][:, 0:nb, :], in_=cS[:, 0:nb, :]), reads=["cS"], writes=["comp0"])
        S.op("dve", lambda: V.tensor_tensor(out=r1[:, 0:nb, :], in0=cS[:, 0:nb, :], in1=comp[0][:, 0:nb, :], op=ALU.subtract),
             reads=["cS", "comp0"], writes=["r1"])
        S.op("dve", lambda: V.tensor_copy(out=comp[1][:, 0:nb, :], in_=r1[:, 0:nb, :]), reads=["r1"], writes=["comp1"])
        S.op("dve", lambda: V.tensor_tensor(out=cS[:, 0:nb, :], in0=r1[:, 0:nb, :], in1=comp[1][:, 0:nb, :], op=ALU.subtract),
             reads=["r1", "comp1"], writes=["cS"])
        S.op("dve", lambda: V.tensor_copy(out=comp[2][:, 0:nb, :], in_=cS[:, 0:nb, :]), reads=["cS"], writes=["comp2"])
        S.op("pool", lambda: G.memset(CA[:, :, :], 0.0), writes=["CA"])
        S.op("pool", lambda: G.memset(CA[:, :, 0:24], 1.0), writes=["CA"])
        for ci in range(3):
            S.op("dve", lambda ci=ci: V.tensor_scalar(out=CA[:, 0:nb, 32 + 8 * ci:40 + 8 * ci], in0=comp[ci][:, 0:nb, :],
                                                      scalar1=-1.0, scalar2=None, op0=ALU.mult),
                 reads=["comp%d" % ci], writes=["CA"])
        for (blk, nr, c0) in blkspec:
            S.op("pe", lambda blk=blk, nr=nr: PE.transpose(out=pT[0:64, 0:nr], in_=CA[0:nr, blk, :], identity=ident[0:nr, 0:nr]),
                 reads=["CA", "ident"], writes=["pT"])
            S.op("dve", lambda nr=nr, c0=c0: V.tensor_copy(out=Aall[:, c0:c0 + nr], in_=pT[0:64, 0:nr]),
                 reads=["pT"], writes=["Aall"])

    def cb_to_B(nblk, nr, dstB):
        S.op("pool", lambda: G.memset(CB[:, :, 24:64], 0.0), writes=["CB"])
        S.op("pool", lambda: G.memset(CB[:, :, 32:56], 1.0), writes=["CB"])
        for j in range(nblk):
            S.op("pe", lambda j=j: PE.transpose(out=pT[0:64, 0:nr], in_=CB[0:nr, j, :], identity=ident[0:nr, 0:nr]),
                 reads=["CB", "ident"], writes=["pT"])
            S.op("dve", lambda j=j: V.tensor_copy(out=dstB[:, j * nr:(j + 1) * nr], in_=pT[0:64, 0:nr]),
                 reads=["pT"], writes=["Ball"])

    P = {"p1": None, "p2": None}
    noop = lambda: None

    def push(s1, s2, s3, s1b=None, s3a=None, s2c=None):
        t2 = {"s1": s1, "s1b": s1b or noop, "s2": s2, "s2c": s2c or noop, "s3a": s3a or noop, "s3": s3}
        t1, t0 = P["p1"], P["p2"]
        t2["s1"]()
        if t1:
            t1["s2"]()
        if t0:
            t0["s3a"]()
        t2["s1b"]()
        if t0:
            t0["s3"]()
        if t1:
            t1["s2c"]()
        P["p2"], P["p1"] = t1, t2

    def flush():
        t1, t0 = P["p1"], P["p2"]
        if t1:
            t1["s2"]()
        if t0:
            t0["s3a"]()
            t0["s3"]()
        if t1:
            t1["s2c"]()
            t1["s3a"]()
            t1["s3"]()
        P["p1"] = P["p2"] = None

    gbs = [(gb, "gb"), (rb, "rb")]

    def attend_head(g, hh, N, qap, qkey, kt_of, v_of, a_of, bap, blocks, pair_first, pair_last):
        nb = len(blocks)

        def mk_tile(bi, nk, blkid, mask):
            first, last = (bi == 0), (bi == nb - 1)
            zi = rot("zb", 2)
            zb, zk = zbs[zi], "zb%d" % zi
            wi = rot("w", 2)
            w_, wk = wt[wi], "wt%d" % wi
            mms = [(kt_of(blkid, nk), qap, ["KT", qkey], 0, N)]
            if mask is not None:
                if g == 1 and len(mask) > 2:
                    m0 = mask[2] // 2
                    mms.append((ident[0:nk, 0:nk], mask[0][:, m0 * 128:(m0 + 1) * 128], ["ident", mask[1]], m0 * 128, (m0 + 1) * 128))
                else:
                    mms.append((ident[0:nk, 0:nk], mask[0], ["ident", mask[1]], 0, N))
            split = False

            def qk():
                for mi, (l_, r_, rd, c0_, c1_) in enumerate(mms):
                    S.op("pe", lambda l_=l_, r_=r_, mi=mi, c0_=c0_, c1_=c1_: PE.matmul(zb[0:nk, c0_:c1_], lhsT=l_, rhs=r_, start=(mi == 0),
                                                                                       stop=(mi == len(mms) - 1)),
                         reads=rd, writes=[zk], inc=(mi == len(mms) - 1) or (split and mi == 0),
                         selfwait=(split and mi == 1))

            def pv():
                S.op("pe", lambda: PE.matmul(ob[:, 0:N], lhsT=v_of(blkid, nk), rhs=w_[0:nk, 0:N],
                                             start=(pair_first and first), stop=(pair_last and last)),
                     reads=[wk, "Vp", "Vp0"], writes=["ob"])

            if g == 0:
                ei = rot("e", 3)
                e_, ek = et[ei], "et%d" % ei
                si = rot("sp", 2)
                sp_, sk = spt[si], "spt%d" % si
                gi = rot("gb", 2)
                gb_, gk = gbs[gi]

                s1 = qk

                def s1b():
                    S.op("act", lambda: SC.activation(out=e_[0:nk, 0:N], in_=zb[0:nk, 0:N], func=AF.Exp),
                         reads=[zk], writes=[ek])

                def s2():
                    S.op("act", lambda: SC.activation(out=sp_[0:nk, 0:N], in_=e_[0:nk, 0:N], func=AF.Ln, bias=1.0),
                         reads=[ek], writes=[sk])
                    S.op("pe", lambda: PE.matmul(gb_[0:nk, 0:N], lhsT=tri[0:nk, 0:nk], rhs=sp_[0:nk, 0:N], start=True,
                                                 stop=first), reads=[sk, "tri"], writes=[gk], inc=first)
                    if not first:
                        S.op("pe", lambda: PE.matmul(gb_[0:nk, 0:N], lhsT=ones[:, 0:nk], rhs=Ssum[:, 0:N], start=False,
                                                     stop=True), reads=["Ssum", "ones"], writes=[gk])

                def s2c():
                    if not last:
                        if first:
                            S.op("dve", lambda: V.memset(Ssum[:, 0:N], 0.0), writes=["Ssum"])
                            S.op("dve", lambda: V.tensor_copy(out=Ssum[0:nk, 0:N], in_=sp_[0:nk, 0:N]),
                                 reads=[sk, "Ssum"], writes=["Ssum"])
                        else:
                            S.op("dve", lambda: V.tensor_tensor(out=Ssum[0:nk, 0:N], in0=Ssum[0:nk, 0:N],
                                                                 in1=sp_[0:nk, 0:N], op=ALU.add),
                                 reads=[sk, "Ssum"], writes=["Ssum"])

                def s3a():
                    S.op("act", lambda: SC.activation(out=Et[0:nk, 0:N], in_=gb_[0:nk, 0:N], func=AF.Exp, scale=-1.0),
                         reads=[gk], writes=["Et"])
                    S.op("dve", lambda: V.tensor_tensor(out=w_[0:nk, 0:N], in0=e_[0:nk, 0:N], in1=Et[0:nk, 0:N],
                                                        op=ALU.mult), reads=[ek, "Et"], writes=[wk])
                s3 = pv
            else:
                s1 = qk

                def s2():
                    for m in range(4):
                        S.op("act", lambda m=m: SC.activation(out=w_[0:nk, m * 128:(m + 1) * 128], in_=zb[0:nk, m * 128:(m + 1) * 128],
                                                              func=AF.Exp, bias=bap[0][0:nk, m, blkid:blkid + 1]),
                             reads=[zk, bap[1]], writes=[wk], inc=(m == 3))
                    if first:
                        S.op("dve", lambda: V.memset(Pacc[:, 0:N], 0.0), writes=["et0"])
                    S.op("dve", lambda: V.tensor_tensor(out=Pacc[0:nk, 0:N], in0=Pacc[0:nk, 0:N], in1=w_[0:nk, 0:N],
                                                         op=ALU.add), reads=[wk, "et0"], writes=["et0"])
                s3 = pv
                s1b = s3a = s2c = None
            push(s1, s2, s3, s1b, s3a, s2c)

        for bi, (nk, blkid, mask) in enumerate(blocks):
            mk_tile(bi, nk, blkid, mask)
        if g == 1:
            def f2():
                S.op("dve", lambda: V.tensor_copy(out=Pbf[:, 0:N], in_=Pacc[:, 0:N]), reads=["et0"], writes=["spt0"])

            def f3():
                S.op("pe", lambda: PE.matmul(rb[:, 0:N], lhsT=o192[:, 64 * hh:64 * hh + 128], rhs=Pbf[:, 0:N],
                                             start=pair_first, stop=pair_last), reads=["spt0", "o192"], writes=["rb"])
            push(noop, f2, f3)

    def attend_multi(g, s_, blocks, maskap, pre_hook=None):
        N = 256
        nb = len(blocks)
        if g == 1:
            for h in range(8):
                S.op("dve", lambda h=h: V.tensor_scalar(out=Bh[0][:, h * 32:(h + 1) * 32], in0=Ball[:, 0:32],
                                                        scalar1=c2[0:64, 2 + h:3 + h], scalar2=None, op0=ALU.mult),
                     reads=["Ball", "c2"], writes=["Bh0"])

        def mk_tile(bi, nk, blkid, has_mask):
            first, last = (bi == 0), (bi == nb - 1)
            zi = rot("zb", 2)
            zb, zk = zbs[zi], "zb%d" % zi
            wi = rot("w", 2)
            w_, wk = wt[wi], "wt%d" % wi
            extra = []
            if g == 1:
                extra.append((Aall[:, 128 * blkid:128 * blkid + nk], Bh[0][:, 0:N], ["Aall", "Bh0"]))
            if has_mask:
                extra.append((ident[0:nk, 0:nk], maskap[0], ["ident", maskap[1]]))

            if not extra:
                extra.append((zr[0:1, 0:nk], zr[0:1, 0:N], ["zr"]))

            def qk():
                for mi, (l_, r_, rd) in enumerate(extra):
                    S.op("pe", lambda l_=l_, r_=r_, mi=mi: PE.matmul(zb[0:nk, 0:N], lhsT=l_, rhs=r_, start=(mi == 0),
                                                                     stop=False), reads=rd, writes=[zk], inc=False)
                for h in (0, 2, 4, 6, 1, 3, 5, 7):
                    p, hh = h // 2, h % 2
                    rows = slice(64 * hh, 64 * hh + 64)
                    S.op("pe", lambda h=h, p=p, rows=rows: PE.matmul(zb[0:nk, h * 32:(h + 1) * 32],
                                                                     lhsT=KT[rows, p, 128 * blkid:128 * blkid + nk],
                                                                     rhs=QTs[rows, p, 32 * s_:32 * s_ + 32],
                                                                     start=False, stop=(h == 7)),
                         reads=["KTs%d" % blkid, "QTs"], writes=[zk], inc=(h in (6, 7)), selfwait=(h == 1))

            def pv():
                if first:
                    S.op("pe", lambda: PE.matmul(ob[:, 0:128], lhsT=zr[0:1, 0:128], rhs=zr[0:1, 0:128], start=True, stop=False),
                         reads=["zr"], writes=["ob"], inc=False)
                for h in range(8):
                    p, hh = h // 2, h % 2
                    S.op("pe", lambda h=h, p=p, hh=hh: PE.matmul(ob[:, p * 32:(p + 1) * 32],
                                                                 lhsT=Vp[0:nk, blkid, p, 64 * hh:64 * hh + 128],
                                                                 rhs=w_[0:nk, h * 32:(h + 1) * 32],
                                                                 start=False, stop=(last and h == 7)),
                         reads=[wk, "Vps%d" % blkid, "Vp0"], writes=["ob"], inc=(h == 7))

            if g == 0:
                ei = rot("e", 3)
                e_, ek = et[ei], "et%d" % ei
                si = rot("sp", 2)
                sp_, sk = spt[si], "spt%d" % si
                gi = rot("gb", 2)
                gb_, gk = gbs[gi]

                def s1b():
                    S.op("act", lambda: SC.activation(out=e_[0:nk, 0:N], in_=zb[0:nk, 0:N], func=AF.Exp),
                         reads=[zk], writes=[ek])

                def s2():
                    S.op("act", lambda: SC.activation(out=sp_[0:nk, 0:N], in_=e_[0:nk, 0:N], func=AF.Ln, bias=1.0),
                         reads=[ek], writes=[sk])
                    S.op("pe", lambda: PE.matmul(gb_[0:nk, 0:N], lhsT=tri[0:nk, 0:nk], rhs=sp_[0:nk, 0:N], start=True,
                                                 stop=first), reads=[sk, "tri"], writes=[gk], inc=first)
                    if not first:
                        S.op("pe", lambda: PE.matmul(gb_[0:nk, 0:N], lhsT=ones[:, 0:nk], rhs=Ssum[:, 0:N], start=False,
                                                     stop=True), reads=["Ssum", "ones"], writes=[gk])

                def s2c():
                    if not last:
                        if first:
                            S.op("dve", lambda: V.memset(Ssum[:, 0:N], 0.0), writes=["Ssum"])
                            S.op("dve", lambda: V.tensor_copy(out=Ssum[0:nk, 0:N], in_=sp_[0:nk, 0:N]),
                                 reads=[sk, "Ssum"], writes=["Ssum"])
                        else:
                            S.op("dve", lambda: V.tensor_tensor(out=Ssum[0:nk, 0:N], in0=Ssum[0:nk, 0:N],
                                                                 in1=sp_[0:nk, 0:N], op=ALU.add),
                                 reads=[sk, "Ssum"], writes=["Ssum"])

                def s3a():
                    S.op("act", lambda: SC.activation(out=Et[0:nk, 0:N], in_=gb_[0:nk, 0:N], func=AF.Exp, scale=-1.0),
                         reads=[gk], writes=["Et"])
                    S.op("dve", lambda: V.tensor_tensor(out=w_[0:nk, 0:N], in0=e_[0:nk, 0:N], in1=Et[0:nk, 0:N],
                                                        op=ALU.mult), reads=[ek, "Et"], writes=[wk])
                push(qk, s2, pv, s1b, s3a, s2c)
            else:
                def s2():
                    S.op("act", lambda: SC.activation(out=w_[0:nk, 0:N], in_=zb[0:nk, 0:N], func=AF.Exp),
                         reads=[zk], writes=[wk])
                    if first:
                        S.op("dve", lambda: V.memset(Pacc[:, 0:N], 0.0), writes=["et0"])
                    S.op("dve", lambda: V.tensor_tensor(out=Pacc[0:nk, 0:N], in0=Pacc[0:nk, 0:N], in1=w_[0:nk, 0:N],
                                                         op=ALU.add), reads=[wk, "et0"], writes=["et0"])
                push(qk, s2, pv)

        for bi, (nk, blkid, has_mask) in enumerate(blocks):
            if pre_hook is not None:
                pre_hook(bi)
            mk_tile(bi, nk, blkid, has_mask)

        def f2():
            if g == 1:
                S.op("dve", lambda: V.tensor_copy(out=Pbf[:, 0:N], in_=Pacc[:, 0:N]), reads=["et0"], writes=["spt0"])

        def f3():
            if g == 1:
                S.op("pe", lambda: PE.matmul(rb[:, 0:128], lhsT=zr[0:1, 0:128], rhs=zr[0:1, 0:128], start=True, stop=False),
                     reads=["zr"], writes=["rb"], inc=False)
                for h in range(8):
                    p, hh = h // 2, h % 2
                    S.op("pe", lambda h=h, p=p, hh=hh: PE.matmul(rb[:, p * 32:(p + 1) * 32], lhsT=o192[:, 64 * hh:64 * hh + 128],
                                                                 rhs=Pbf[:, h * 32:(h + 1) * 32], start=False, stop=(h == 7)),
                         reads=["spt0", "o192"], writes=["rb"], inc=(h == 7))
            o3 = ob[:, 0:128].rearrange("p (a q) -> p a q", a=4)
            sg3 = SGs[:, :, 32 * s_:32 * s_ + 32]
            dst = smix[:, g * 4:(g + 1) * 4, 32 * s_:32 * s_ + 32]
            if g == 1:
                r3 = rcp[:, 0:128].rearrange("p (a q) -> p a q", a=4)
                S.op("dve", lambda: V.reciprocal(out=rcp[:, 0:128], in_=rb[:, 0:128]), reads=["rb"], writes=["Et"])
                S.op("dve", lambda: V.tensor_tensor(out=rcp[:, 0:128], in0=ob[:, 0:128], in1=rcp[:, 0:128], op=ALU.mult),
                     reads=["ob", "Et"], writes=["Et"])
                S.op("dve", lambda: V.tensor_tensor(out=dst, in0=r3, in1=sg3, op=ALU.mult), reads=["Et", "SGs"], writes=["smix"])
            else:
                S.op("dve", lambda: V.tensor_tensor(out=dst, in0=o3, in1=sg3, op=ALU.mult), reads=["ob", "SGs"], writes=["smix"])
        push(noop, f2, f3)

    def finish_pair(g, N, sgap, sgkey, dst, dkey):
        if g == 1:
            S.op("dve", lambda: V.reciprocal(out=rcp[:, 0:N], in_=rb[:, 0:N]), reads=["rb"], writes=["Et"])
            S.op("dve", lambda: V.tensor_tensor(out=rcp[:, 0:N], in0=ob[:, 0:N], in1=rcp[:, 0:N], op=ALU.mult),
                 reads=["ob", "Et"], writes=["Et"])
            S.op("dve", lambda: V.tensor_tensor(out=dst, in0=rcp[:, 0:N], in1=sgap, op=ALU.mult),
                 reads=["Et", sgkey], writes=[dkey])
        else:
            S.op("dve", lambda: V.tensor_tensor(out=dst, in0=ob[:, 0:N], in1=sgap, op=ALU.mult),
                 reads=["ob", sgkey], writes=[dkey])

    def qg_proj(xT, n, xkey, QTd, SGd, qkey, sgkey):
        for p in range(4):
            p_, pk_ = proj_fm(p * 128, xT, n, xkey)
            S.op("dve", lambda p_=p_, p=p: V.tensor_copy(out=QTd[:, p, 0:n], in_=p_[:, 0:n]), reads=[pk_], writes=[qkey])
        for p in range(4):
            p_, pk_ = proj_fm(512 + p * 128, xT, n, xkey)
            S.op("act", lambda p_=p_, p=p: SC.activation(out=SGd[:, p, 0:n], in_=p_[:, 0:n], func=AF.Silu),
                 reads=[pk_], writes=[sgkey])

    try:
        _stage(0)
        for g in range(2):
            gc = g * 2048
            for r in range(8):
                gi = rot("gen", 2)
                S.dma("sp", gen[gi][:, 0:512], negm[g * 8 + r], writes=["gen%d" % gi], key="ldw%d" % gi)
                S.op("pool", lambda gi=gi, r=r: G.tensor_copy(out=negb[:, r, :], in_=gen[gi][:, 0:512]),
                     reads=["gen%d" % gi], writes=["negb"])
            if g == 1:
                S.op("dve", lambda: V.memset(maskc[:, :, :], 0.0), writes=["maskc"])
                for r in range(8):
                    for m in range(4):
                        if m != r // 2:
                            S.op("dve", lambda r=r, m=m: V.tensor_copy(out=maskc[:, m, r:r + 1], in_=negb[:, r, m * 128:m * 128 + 1]),
                                 reads=["negb", "maskc"], writes=["maskc"])
            _stage(1)
            load_w([(gc + 512, 512), (gc + 1024, 512)] + ([(4096, 8)] if g == 1 else []))
            _stage(2)
            for ch in range(-1, 8):
                nt = 16 if ch < 0 else 512
                nbk = 1 if ch < 0 else 4
                tok0 = 0 if ch < 0 else 16 + 512 * ch
                for j in range(nbk):
                    n = min(128, nt)
                    src = meta if ch < 0 else xall[ch * 512 + j * 128: ch * 512 + (j + 1) * 128, :]
                    norm_T(src, n, xnT[:, :, j * 128:j * 128 + n], "xnT")
                for p in range(4):
                    p_, pk_ = proj_fm(p * 128, xnT, nt, "xnT")
                    S.op("act", lambda p_=p_, p=p, tok0=tok0, nt=nt: SC.activation(out=KT[:, p, tok0:tok0 + nt], in_=p_[:, 0:nt], func=AF.Copy, scale=0.125),
                         reads=[pk_], writes=["KT"])
                for j in range(nbk):
                    n = min(128, nt)
                    vblk = 0 if ch < 0 else 1 + ch * 4 + j
                    tm_kv_out(xnT, j * 128, n, "xnT", g, vblk, tok0 + j * 128, pko)
                    if g == 1:
                        logf_out(xnT, j * 128, n, "xnT", lfa[0:n, vblk, :], tok0 + j * 128, plf)
            if g == 1:
                decay_prep(33, [(0, 16, 0)] + [(1 + j, 128, 16 + 128 * j) for j in range(32)])
                pv_ = pre[:, 2:34, :].rearrange("p (m two) h -> p m two h", two=2)
                S.op("dve", lambda: V.tensor_scalar(out=t1[:, :, :], in0=pv_[:, :, 0, :], scalar1=c2[:, 0:1], scalar2=None,
                                                    op0=ALU.mult), reads=["pre", "c2"], writes=["t1"])
                S.op("dve", lambda: V.scalar_tensor_tensor(out=crefO[:, :, :], in0=pv_[:, :, 1, :], scalar=c2[:, 1:2],
                                                           in1=t1[:, :, :], op0=ALU.mult, op1=ALU.add),
                     reads=["t1", "c2", "pre"], writes=["crefO"])
            _stage(3 + 10 * g)
            norm_T(xs, 64, xnTs[:, :, 0:64], "xnTs")
            for p in range(4):
                p_, pk_ = proj_fm(p * 128, xnTs, 64, "xnTs")
                S.op("act", lambda p_=p_, p=p: SC.activation(out=KTs[:, p, :], in_=p_[:, 0:64], func=AF.Copy, scale=0.125), reads=[pk_], writes=["KTs"])
            for s_ in range(2):
                S.op("pool", lambda s_=s_: G.memset(svt[s_][:, :, :], 0.0), writes=["svt%d" % s_])
                for wi, which in enumerate(("k", "v")):
                    p_, pk_ = proj_tm(wi * 512, 512, xnTs, 32 * s_, 32, "xnTs")
                    si = rot("stg", 3)
                    sk = "stg%d" % si
                    S.op("dve", lambda p_=p_, si=si: V.tensor_copy(out=stg[si][0:32, :], in_=p_[0:32, :]), reads=[pk_], writes=[sk])
                    S.dma("sp", sko[(which, g)][32 * s_:32 * s_ + 32, :], stg[si][0:32, :], reads=[sk], key="st%d" % si)
                    if which == "v":
                        sv = stg[si][0:32, :].rearrange("t (p two d) -> t p two d", two=2, d=64)
                        S.op("pool", lambda sv=sv, s_=s_: G.tensor_copy(out=svt[s_][:, :, 0:64], in_=sv[:, :, 0, :]),
                             reads=[sk], writes=["svt%d" % s_])
                        S.op("pool", lambda sv=sv, s_=s_: G.tensor_copy(out=svt[s_][:, :, 128:192], in_=sv[:, :, 1, :]),
                             reads=[sk], writes=["svt%d" % s_])
                if g == 1:
                    logf_out(xnTs, 32 * s_, 32, "xnTs", slt[s_][:, :], 32 * s_, slf, dkey="slt%d" % s_)
            _stage(4 + 10 * g)
            load_w([(gc, 512), (gc + 1536, 512)])
            qg_proj(xnTs, 64, "xnTs", QTs, SGs, "QTs", "SGs")
            _stage(5 + 10 * g)
            for i in range(4):
                for j in range(4):
                    norm_T(xown[i * 512 + j * 128: i * 512 + (j + 1) * 128, :], 128, xnT[:, :, j * 128:(j + 1) * 128], "xnT")
                qg_proj(xnT, 512, "xnT", QT, SG, "QT", "SG")
                blocks = [(128, 1 + jb, (negb[:, jb - 8 * i, :], "negb", jb - 8 * i) if jb >= 8 * i else None)
                          for jb in range(8 * i + 7, -1, -1)] + [(16, 0, None)]
                for p in range(4):
                    for hh in range(2):
                        h = 2 * p + hh
                        rows = slice(64 * hh, 64 * hh + 64)
                        bap = None
                        if g == 1:
                            bi_ = rot("bh", 2)
                            for m in range(4):
                                S.op("dve", lambda bi_=bi_, h=h, i=i, m=m: V.tensor_scalar(out=biasq[bi_][:, m, :], in0=negc[:, :, h],
                                                                                           scalar1=crefO[:, 4 * i + m, h:h + 1],
                                                                                           scalar2=None, op0=ALU.add),
                                     reads=["negc", "crefO"], writes=["biasq%d" % bi_])
                            S.op("dve", lambda bi_=bi_, i=i: V.tensor_tensor(out=biasq[bi_][:, :, 1 + 8 * i:9 + 8 * i],
                                                                             in0=biasq[bi_][:, :, 1 + 8 * i:9 + 8 * i],
                                                                             in1=maskc[:, :, :], op=ALU.add),
                                 reads=["maskc", "biasq%d" % bi_], writes=["biasq%d" % bi_])
                            bap = (biasq[bi_], "biasq%d" % bi_)

                        def kt_of(blkid, nk, p=p, rows=rows):
                            c0 = 0 if blkid == 0 else 16 + 128 * (blkid - 1)
                            return KT[rows, p, c0:c0 + nk]

                        def v_of(blkid, nk, p=p, hh=hh):
                            return Vp[0:nk, blkid, p, 64 * hh:64 * hh + 128]

                        def a_of(blkid, nk):
                            c0 = 0 if blkid == 0 else 16 + 128 * (blkid - 1)
                            return Aall[:, c0:c0 + nk]

                        _stage(5 + 10 * g + 0.1)
                        attend_head(g, hh, 512, QT[rows, p, :], "QT", kt_of, v_of, a_of, bap, blocks, hh == 0, hh == 1)
                        _stage(5 + 10 * g + 0.2)
                    mi = rot("mix", 2)

                    def fin(mi=mi, p=p, i=i, g=g):
                        finish_pair(g, 512, SG[:, p, :], "SG", mixb[mi][:, :], "mixb%d" % mi)
                        S.dma("sp", mixT[g * 512 + p * 128: g * 512 + (p + 1) * 128, i * 512:(i + 1) * 512], mixb[mi][:, :],
                              reads=["mixb%d" % mi], key="stm%d" % mi)
                    push(noop, noop, fin)
                flush()
            _stage(6 + 10 * g)
            for h in range(8):
                S.op("dve", lambda h=h, g=g: V.tensor_copy(out=negs8[:, h * 32:(h + 1) * 32], in_=negs[0:32, 32 * g:32 * g + 32]),
                     reads=["negs"], writes=["negs8"])
            for s_ in range(2):
                def prep_blk(blk, s_=s_, g=g):
                    kk, vk = "KTs%d" % blk, "Vps%d" % blk
                    if blk == 32:
                        S.op("act", lambda: SC.copy(out=KT[:, :, 4096:4128], in_=KTs[:, :, 32 * s_:32 * s_ + 32]),
                             reads=["KTs"], writes=["KT", kk])
                        S.op("pool", lambda: G.tensor_copy(out=Vp[0:32, 32, :, :], in_=svt[s_][:, :, :]),
                             reads=["svt%d" % s_], writes=["Vp", vk])
                        return
                    si = rot("stg", 3)
                    S.dma("sp", stg[si][:, :], cach[("k", g)][s_, blk * 128:(blk + 1) * 128, :], writes=["stg%d" % si], key="st%d" % si)
                    ci_ = rot("mix", 2)
                    S.op("dve", lambda: V.tensor_copy(out=ckb[ci_][:, :], in_=stg[si][:, :]),
                         reads=["stg%d" % si], writes=["mixb%d" % ci_])
                    for p in range(4):
                        S.op("pe", lambda p=p: PE.transpose(out=pT[:, p * 128:(p + 1) * 128],
                                                            in_=ckb[ci_][:, p * 128:(p + 1) * 128], identity=ident[:, :]),
                             reads=["mixb%d" % ci_, "ident"], writes=["pT"], inc=(p == 3))
                    S.op("act", lambda: SC.activation(out=KT[:, :, blk * 128:(blk + 1) * 128],
                                                      in_=pT[:, 0:512].rearrange("p (q t) -> p q t", q=4), func=AF.Copy, scale=0.125),
                         reads=["pT"], writes=["KT", kk])
                    si2 = rot("stg", 3)
                    S.dma("sp", stg[si2][:, :], cach[("v", g)][s_, blk * 128:(blk + 1) * 128, :], writes=["stg%d" % si2], key="st%d" % si2)
                    sv = stg[si2][:, :].rearrange("t (p two d) -> t p two d", two=2, d=64)
                    S.op("pool", lambda: G.tensor_copy(out=Vp[:, blk, :, 0:64], in_=sv[:, :, 0, :]),
                         reads=["stg%d" % si2], writes=["Vp", vk], inc=False)
                    S.op("pool", lambda: G.tensor_copy(out=Vp[:, blk, :, 128:192], in_=sv[:, :, 1, :]),
                         reads=["stg%d" % si2], writes=["Vp", vk])

                if g == 1:
                    S.dma("sp", lfa[:, 0:32, :], clf[s_].rearrange("(b p) h -> p b h", p=128), reads=[], writes=["lfa"], key="ldlf")
                    S.op("pool", lambda: G.memset(lfa[:, 32, :], 0.0), reads=[], writes=["lfa"])
                    S.op("pool", lambda s_=s_: G.tensor_copy(out=lfa[0:32, 32, :], in_=slt[s_][:, :]),
                         reads=["slt%d" % s_, "lfa"], writes=["lfa"])
                    decay_prep(33, [(j, 128, 128 * j) for j in range(32)] + [(32, 32, 4096)])
                    for ci in range(3):
                        S.op("dve", lambda ci=ci: V.tensor_copy(out=CB[0:32, 0, 8 * ci:8 * ci + 8], in_=comp[ci][0:32, 32, :]),
                             reads=["comp%d" % ci], writes=["CB"])
                    cb_to_B(1, 32, Ball)
                blocks = [(32, 32, True)] + [(128, jb, False) for jb in range(31, -1, -1)]
                order = [blk for (_, blk, _) in blocks]
                emitted = [0]
                LAG = 6

                def pre_hook(bi, order=order, emitted=emitted, prep_blk=prep_blk):
                    tgt = min(len(order), bi + 1 + LAG)
                    while emitted[0] < tgt:
                        prep_blk(order[emitted[0]])
                        emitted[0] += 1
                attend_multi(g, s_, blocks, (negs8[0:32, :], "negs8"), pre_hook)
                flush()
                S.op("pe", lambda: PE.matmul(rb[0:1, 0:1], lhsT=zr[0:1, 0:1], rhs=zr[0:1, 0:1], start=True, stop=True),
                     reads=["KTs%d" % b_ for b_ in range(33)] + ["Vps%d" % b_ for b_ in range(33)] + ["zr"],
                     writes=["KT", "Vp", "rb"])

        _stage(30)
        load_w([(0, 1024)], src=w_out, fold=False)
        mview = mixT.rearrange("(k p) t -> p k t", p=128)

        def out_block(mt_ap, mkey, xsrc, n, ydst):
            xi_i = rot("xin", 2)
            xi, xk = xin[xi_i], "xin%d" % xi_i
            S.dma("sp", xi[0:n, :], xsrc, writes=[xk], key="ldx%d" % xi_i)
            for half in range(2):
                bank, bkey = next_pp()
                for k in range(8):
                    S.op("pe", lambda k=k, bank=bank, half=half: PE.matmul(bank[0:n, :], lhsT=mt_ap[:, k, 0:n],
                                                                           rhs=Wbf[:, k, half * 512:(half + 1) * 512],
                                                                           start=(k == 0), stop=(k == 7)),
                         reads=["Wbf", mkey], writes=[bkey], inc=(k == 7))
                S.op("dve", lambda bank=bank, half=half: V.tensor_tensor(out=xi[0:n, half * 512:(half + 1) * 512], in0=bank[0:n, :],
                                                                         in1=xi[0:n, half * 512:(half + 1) * 512], op=ALU.add),
                     reads=[bkey, xk], writes=[xk])
            rstd_of(xi, n, xk)
            gi = rot("gen", 2)
            S.op("dve", lambda gi=gi: V.scalar_tensor_tensor(out=gen[gi][0:n, 0:1024], in0=xi[0:n, :], scalar=rs2[0:n, 0:1],
                                                             in1=fgt[0:n, :], op0=ALU.mult, op1=ALU.mult),
                 reads=[xk, "rs2", "fgt"], writes=["gen%d" % gi])
            S.dma("sp", ydst, gen[gi][0:n, 0:1024], reads=["gen%d" % gi], key="ldw%d" % gi)

        for blk in range(16):
            mi = rot("mT", 2)
            mt_ = xnT[:, :, mi * 128:(mi + 1) * 128]
            mk_ = "xnTm%d" % mi
            S.dma("sp", mt_, mview[:, :, blk * 128:(blk + 1) * 128], writes=([mk_, "xnT"] if blk < 2 else [mk_]), key="ldm%d" % mi)
            out_block(mt_, mk_, xown[blk * 128:(blk + 1) * 128, :], 128, y_own[blk * 128:(blk + 1) * 128, :])
        out_block(smix, "smix", xs, 64, ys)


    except _Stop:
        pass
    S.emit()
    es.close()
    return nc


def _consts():
    c = np.zeros((128, CW), np.float32)
    j = np.arange(128)
    c[:, C_ID:C_ID + 128] = np.eye(128)
    c[:, C_TRI:C_TRI + 128] = (j[:, None] >= j[None, :])
    c[:, C_ONE:C_ONE + 128] = 1.0
    c[:, C_TLE:C_TLE + 128] = (j[:, None] <= j[None, :])
    c[:, C_O192:C_O192 + 64] = 1.0
    c[:, C_O192 + 128:C_O192 + 192] = 1.0
    k = np.arange(32)
    c[0:32, C_NS:C_NS + 32] = np.where(k[:, None] < k[None, :], 0.0, NEGV)
    c[0:32, C_NS + 32:C_NS + 64] = np.where(k[:, None] <= k[None, :], 0.0, NEGV)
    c[:, C_NH] = -0.5
    return c


def _masks(par):
    m = np.zeros((16, 128, 512), np.float32)
    s = np.arange(128)[:, None]
    for r in range(8):
        for mm in range(4):
            kpos = r * 128 + s
            qpos = (par + 2 * mm) * 128 + np.arange(128)[None, :]
            m[r, :, mm * 128:(mm + 1) * 128] = np.where(kpos < qpos, 0.0, NEGV)
            m[8 + r, :, mm * 128:(mm + 1) * 128] = np.where(kpos <= qpos, 0.0, NEGV)
    return m


_NC = None
_SIM_HOOK = None


def kernel(x_prompt, x_sample, cache_a_k, cache_a_v, cache_b_k, cache_b_v, cache_b_logf,
           meta_tokens, norm_g, w_in, b_f, w_out, final_g):
    global _NC
    f = lambda a: np.ascontiguousarray(np.asarray(a, dtype=np.float32))
    x_prompt, x_sample = f(x_prompt), f(x_sample)
    cak, cav, cbk, cbv, clf = f(cache_a_k), f(cache_a_v), f(cache_b_k), f(cache_b_v), f(cache_b_logf)
    meta_tokens, norm_g, w_in, b_f, w_out, final_g = f(meta_tokens), f(norm_g), f(w_in), f(b_f), f(w_out), f(final_g)
    if _NC is None:
        _NC = build()
    nc = _NC
    cst = _consts()
    in_maps = []
    for core in range(8):
        b, c = core // 2, core % 2
        c2 = np.zeros((128, 16), np.float32)
        c2[:, 0] = 1.0 if c == 0 else 0.0
        c2[:, 1] = 1.0 if c == 1 else 0.0
        for h in range(8):
            for cc in range(3):
                c2[cc * 8 + h, 2 + h] = 1.0
                c2[32 + cc * 8 + h, 2 + h] = 1.0
        sl = slice(2 * core, 2 * core + 2)
        in_maps.append({
            "xall": x_prompt[b],
            "meta": meta_tokens,
            "xown": f(x_prompt[b].reshape(16, 2, 128, 1024)[:, c].reshape(2048, 1024)),
            "xs": f(x_sample[sl].reshape(64, 1024)),
            "w_in": w_in[0], "w_out": w_out[0],
            "ng": f(norm_g[0].reshape(8, 128).T),
            "fg": f(np.broadcast_to(final_g[None, :], (128, 1024))),
            "bfb": f(np.broadcast_to(b_f[0][None, :], (128, 8))),
            "cak": f(cak[0, sl].reshape(2, 4096, 512)), "cav": f(cav[0, sl].reshape(2, 4096, 512)),
            "cbk": f(cbk[0, sl].reshape(2, 4096, 512)), "cbv": f(cbv[0, sl].reshape(2, 4096, 512)),
            "clf": f(clf[0, sl]),
            "cst": cst, "cst2": c2, "negm": _masks(c),
        })
    if _SIM_HOOK is not None:
        return _SIM_HOOK(nc, in_maps)
    res = run_bass_kernel_spmd(nc, in_maps, core_ids=list(range(8)))
    return _assemble(res.results)


def _assemble(R):
    y_prompt = np.zeros((4, 4096, 1024), np.float32)
    y_sample = np.zeros((16, 32, 1024), np.float32)
    pk = {k: np.zeros((1, 4, 4112, 8, 64), np.float32) for k in ("pka", "pva", "pkb", "pvb")}
    p_lf = np.zeros((1, 4, 4112, 8), np.float32)
    sk = {k: np.zeros((1, 16, 32, 8, 64), np.float32) for k in ("sak", "sav", "sbk", "sbv")}
    s_lf = np.zeros((1, 16, 32, 8), np.float32)
    for core in range(8):
        b, c = core // 2, core % 2
        r = R[core]
        y_prompt[b].reshape(16, 2, 128, 1024)[:, c] = np.asarray(r["y_own"]).reshape(16, 128, 1024)
        y_sample[2 * core:2 * core + 2] = np.asarray(r["ys"]).reshape(2, 32, 1024)
        if c == 0:
            for k in pk:
                pk[k][0, b] = np.asarray(r[k]).reshape(4112, 8, 64)
            p_lf[0, b] = np.asarray(r["plf"])
        for k in sk:
            sk[k][0, 2 * core:2 * core + 2] = np.asarray(r[k]).reshape(2, 32, 8, 64)
        s_lf[0, 2 * core:2 * core + 2] = np.asarray(r["slf"]).reshape(2, 32, 8)
    return (y_prompt, y_sample, pk["pka"], pk["pva"], pk["pkb"], pk["pvb"], p_lf,
            sk["sak"], sk["sav"], sk["sbk"], sk["sbv"], s_lf)
```

```python
import os
import numpy as np
from contextlib import ExitStack
import concourse.bass as bass
import concourse.mybir as mybir
from concourse.bass_utils import run_bass_kernel_spmd

F32 = mybir.dt.float32
BF16 = mybir.dt.bfloat16
AF = mybir.ActivationFunctionType
ALU = mybir.AluOpType
NEGV = -30000.0
EPS = 1e-6

C_ID, C_TRI, C_ONE, C_TLE, C_O192, C_NS, C_NH = 0, 128, 256, 384, 512, 704, 768
CW = 769


class _Stop(Exception):
    pass


def _stage(n):
    if float(os.environ.get('KSTAGE', '99')) <= n:
        raise _Stop()


class Sched:
    def __init__(self, nc, es):
        self.nc, self.es = nc, es
        self.q = {k: [] for k in ("pe", "act", "dve", "pool", "sp")}
        self.sem, self.cnt = {}, {}
        self.seen = {k: {} for k in self.q}
        self.st = {}
        for k in self.q:
            self._mk(k)

    def _mk(self, key):
        self.sem[key] = self.es.enter_context(self.nc.semaphore("s_" + key))
        self.cnt[key] = 0

    def _deps(self, e, reads, writes):
        deps = []
        for b in reads:
            s = self.st.get(b)
            if s and s["w"]:
                deps.append(s["w"])
        for b in writes:
            s = self.st.get(b)
            if s:
                if s["w"]:
                    deps.append(s["w"])
                deps.extend(s["r"].items())
        waits = []
        for (k, v) in deps:
            if k == e and (e == "pe" or v > self.cnt[e]):
                continue
            if e == "sp" and k not in self.q:
                v = self.cnt[k]
            if self.seen[e].get(k, 0) < v:
                self.seen[e][k] = v
                waits.append((k, v))
        return waits

    def _mark(self, key, val, reads, writes):
        for b in reads:
            s = self.st.setdefault(b, {"w": None, "r": {}})
            s["r"][key] = max(s["r"].get(key, 0), val)
        for b in writes:
            self.st[b] = {"w": (key, val), "r": {}}

    def op(self, e, fn, reads=(), writes=(), inc=True, selfwait=False):
        waits = self._deps(e, reads, writes)
        if selfwait and self.cnt[e] > self.seen[e].get(e, 0):
            self.seen[e][e] = self.cnt[e]
            waits.append((e, self.cnt[e]))
        val = self.cnt[e] + 1
        if inc:
            self.cnt[e] = val
        self.q[e].append((waits, fn, (e, 1) if inc else None))
        self._mark(e, val, reads, writes)

    def dma(self, qe, out, in_, reads=(), writes=(), key=None):
        if key not in self.sem:
            self._mk(key)
        waits = self._deps(qe, reads, writes)
        self.cnt[key] += 16
        eng = {"sp": self.nc.sync, "pool": self.nc.gpsimd, "act": self.nc.scalar}[qe]
        self.q[qe].append((waits, lambda: eng.dma_start(out=out, in_=in_), (key, 16)))
        self._mark(key, self.cnt[key], reads, writes)

    def emit(self):
        nc = self.nc
        fin = [(k, v) for k, v in self.cnt.items() if k not in self.q and v > 0]
        self.q["sp"].append((fin, None, None))
        for e in ("act",):
            self.q[e].append(([(k, self.cnt[k]) for k in ("pe", "dve", "pool") if self.cnt[k] > 0], None, None))

        def mk(k):
            def body(eng):
                for waits, fn, inc in self.q[k]:
                    for (sk, v) in waits:
                        eng.wait_ge(self.sem[sk], v)
                    if fn is None:
                        continue
                    ins = fn()
                    if inc:
                        ins.then_inc(self.sem[inc[0]], inc[1])
            return body

        with nc.Block() as block:
            block.tensor(mk("pe"))
            block.scalar(mk("act"))
            block.vector(mk("dve"))
            block.gpsimd(mk("pool"))
            block.sync(mk("sp"))


def build():
    nc = bass.Bass("TRN2", target_bir_lowering=False)
    es = ExitStack()
    S = Sched(nc, es)

    def din(n, shp, dt=F32):
        return nc.dram_tensor(n, list(shp), dt, kind="ExternalInput").ap()

    def dout(n, shp, dt=F32):
        return nc.dram_tensor(n, list(shp), dt, kind="ExternalOutput").ap()

    def sb(n, shp, dt=F32):
        return es.enter_context(nc.sbuf_tensor(n, list(shp), dt))

    def ps(n, shp, dt=F32):
        return es.enter_context(nc.psum_tensor(n, list(shp), dt))

    xall = din("xall", [4096, 1024]); meta = din("meta", [16, 1024]); xown = din("xown", [2048, 1024])
    xs = din("xs", [64, 1024])
    w_in = din("w_in", [1024, 4104]); w_out = din("w_out", [1024, 1024])
    ng = din("ng", [128, 8]); fg = din("fg", [128, 1024]); bfb = din("bfb", [128, 8])
    cach = {("k", 0): din("cak", [2, 4096, 512]), ("v", 0): din("cav", [2, 4096, 512]),
            ("k", 1): din("cbk", [2, 4096, 512]), ("v", 1): din("cbv", [2, 4096, 512])}
    clf = din("clf", [2, 4096, 8])
    cst = din("cst", [128, CW]); cst2 = din("cst2", [128, 16]); negm = din("negm", [16, 128, 512])

    y_own = dout("y_own", [2048, 1024]); ys = dout("ys", [64, 1024])
    pko = {("k", 0): dout("pka", [4112, 512]), ("v", 0): dout("pva", [4112, 512]),
           ("k", 1): dout("pkb", [4112, 512]), ("v", 1): dout("pvb", [4112, 512])}
    plf = dout("plf", [4112, 8])
    sko = {("k", 0): dout("sak", [64, 512]), ("v", 0): dout("sav", [64, 512]),
           ("k", 1): dout("sbk", [64, 512]), ("v", 1): dout("sbv", [64, 512])}
    slf = dout("slf", [64, 8])
    mixT = nc.dram_tensor("mixT", [1024, 2048], BF16).ap()

    cstf = sb("cstf", [128, CW]); c2 = sb("c2", [128, 16])
    ident = sb("ident", [128, 128], BF16); tri = sb("tri", [128, 128], BF16)
    ones = sb("ones", [128, 128], BF16); o192 = sb("o192", [128, 192], BF16)
    negs = sb("negs", [128, 64], BF16)
    negs8 = sb("negs8", [32, 256], BF16)
    zr = sb("zr", [1, 256], BF16)
    ngt = sb("ngt", [128, 8]); fgt = sb("fgt", [128, 1024]); bft = sb("bft", [128, 8])
    negb = sb("negb", [128, 8, 512], BF16)
    gen = [sb("gen%d" % i, [128, 1032]) for i in range(2)]
    Wbf = sb("Wbf", [128, 8, 1032], BF16)
    xin = [sb("xin%d" % i, [128, 1024]) for i in range(2)]
    xnb = sb("xnb", [128, 1024], BF16)
    ssqs = [sb("ssq%d" % i, [128, 1]) for i in range(2)]; rss = [sb("rs%d" % i, [128, 1]) for i in range(2)]
    rs2s = [sb("rs2_%d" % i, [128, 1]) for i in range(2)]
    xnT = sb("xnT", [128, 8, 512], BF16)
    xnTs = sb("xnTs", [128, 8, 64], BF16)
    KT = sb("KT", [128, 4, 4128], BF16)
    Vp = sb("Vp", [128, 33, 4, 192], BF16)
    QT = sb("QT", [128, 4, 512], BF16); SG = sb("SG", [128, 4, 512], BF16)
    KTs = sb("KTs", [128, 4, 64], BF16); QTs = sb("QTs", [128, 4, 64], BF16); SGs = sb("SGs", [128, 4, 64], BF16)
    smix = sb("smix", [128, 8, 64], BF16)
    stg = [sb("stg%d" % i, [128, 512]) for i in range(3)]
    et = [sb("et%d" % i, [128, 512]) for i in range(3)]
    spt = [sb("spt%d" % i, [128, 512], BF16) for i in range(2)]
    Ssum = sb("Ssum", [128, 512], BF16)
    Et = sb("Et", [128, 512])
    wt = [sb("wt%d" % i, [128, 512], BF16) for i in range(2)]
    Pacc, Pbf, rcp = et[0], spt[0], Et
    mixb = [sb("mixb%d" % i, [128, 512], BF16) for i in range(2)]
    ckb = mixb
    svt = [sb("svt%d" % s_, [32, 4, 192], BF16) for s_ in range(2)]
    slt = [sb("slt%d" % s_, [32, 8]) for s_ in range(2)]
    lfa = sb("lfa", [128, 33, 8]); totS = sb("totS", [128, 33, 8]); pre = sb("pre", [128, 34, 8])
    negc = sb("negc", [128, 33, 8]); crefO = sb("crefO", [128, 16, 8]); maskc = sb("maskc", [128, 4, 8])
    biasq = [sb("biasq%d" % i, [128, 4, 33]) for i in range(2)]
    cS = sb("cS", [128, 33, 8]); r1 = sb("r1", [128, 33, 8])
    comp = [sb("comp%d" % i, [128, 33, 8], BF16) for i in range(3)]
    CA = sb("CA", [128, 33, 64], BF16); CB = sb("CB", [128, 1, 64], BF16); t1 = sb("t1", [128, 16, 8])
    Aall = sb("Aall", [64, 4128], BF16); Ball = sb("Ball", [64, 64], BF16)
    Bh = [sb("Bh%d" % i, [64, 512], BF16) for i in range(2)]
    lft = sb("lft", [128, 8]); lft2 = sb("lft2", [128, 8])

    pT = ps("pT", [128, 1024], BF16)
    pp = [ps("pp%d" % i, [128, 512]) for i in range(2)]
    zbs = [ps("zb%d" % i, [128, 512]) for i in range(2)]
    gb = ps("gb", [128, 512]); ob = ps("ob", [128, 512]); rb = ps("rb", [128, 512])

    V, SC, G, PE = nc.vector, nc.scalar, nc.gpsimd, nc.tensor
    ctr = {"rs": 0, "ppx": 0, "sp": 0, "gb": 0, "pp": 0, "zb": 0, "stg": 0, "xin": 0, "gen": 0, "e": 0, "w": 0, "mix": 0, "bh": 0, "ckb": 0, "mT": 0}

    def rot(name, n):
        i = ctr[name] % n
        ctr[name] += 1
        return i

    S.dma("sp", cstf[:, :], cst, writes=["cstf"], key="c0")
    S.dma("sp", c2[:, :], cst2, writes=["c2"], key="c1")
    S.dma("sp", ngt[:, :], ng, writes=["ngt"], key="c2")
    S.dma("sp", fgt[:, :], fg, writes=["fgt"], key="c3")
    S.dma("sp", bft[:, :], bfb, writes=["bft"], key="c4")
    for dst, off, wd, nm in ((ident, C_ID, 128, "ident"), (tri, C_TRI, 128, "tri"), (ones, C_ONE, 128, "ones"),
                             (o192, C_O192, 192, "o192"), (negs, C_NS, 64, "negs")):
        S.op("dve", lambda dst=dst, off=off, wd=wd: V.tensor_copy(out=dst[:, :], in_=cstf[:, off:off + wd]),
             reads=["cstf"], writes=[nm])
    tle_f = cstf[:, C_TLE:C_TLE + 128]
    one_f = cstf[:, C_ONE:C_ONE + 128]
    nh_f = cstf[:, C_NH:C_NH + 1]
    S.op("pool", lambda: G.memset(Vp[:, :, :, 64:128], 0.0), writes=["Vp0"])
    S.op("pool", lambda: G.memset(lfa[:, :, :], 0.0), writes=["lfa"])
    S.op("pool", lambda: G.memset(zr[:, :], 0.0), writes=["zr"])

    def load_w(colspecs, src=w_in, fold=True):
        for k in range(8):
            gi = rot("gen", 2)
            o = 0
            for (c0, wd) in colspecs:
                S.dma("sp", gen[gi][:, o:o + wd], src[k * 128:(k + 1) * 128, c0:c0 + wd],
                      writes=["gen%d" % gi], key="ldw%d" % gi)
                o += wd
            if fold:
                S.op("dve", lambda gi=gi, k=k, o=o: V.tensor_scalar(out=Wbf[:, k, 0:o], in0=gen[gi][:, 0:o],
                                                                   scalar1=ngt[:, k:k + 1], scalar2=None, op0=ALU.mult),
                     reads=["gen%d" % gi, "ngt"], writes=["Wbf"])
            else:
                S.op("dve", lambda gi=gi, k=k, o=o: V.tensor_copy(out=Wbf[:, k, 0:o], in_=gen[gi][:, 0:o]),
                     reads=["gen%d" % gi], writes=["Wbf"])

    def rstd_of(src_t, n, key, junk=None):
        ri = rot("rs", 2)
        ssq, rs, rs2 = ssqs[ri], rss[ri], rs2s[ri]
        jt, jk = junk if junk is not None else (xnb, "xnb")
        S.op("act", lambda: SC.activation(out=jt[0:n, 0:1024], in_=src_t[0:n, :], func=AF.Square, accum_out=ssq[0:n, 0:1]),
             reads=[key], writes=[jk, "ssq%d" % ri])
        S.op("dve", lambda: V.tensor_scalar(out=rs[0:n, :], in0=ssq[0:n, :], scalar1=1.0 / 1024.0, scalar2=EPS,
                                            op0=ALU.mult, op1=ALU.add), reads=["ssq%d" % ri], writes=["rs%d" % ri])
        S.op("pool", lambda: G.tensor_tensor(out=rs2[0:n, :], in0=rs[0:n, :], in1=nh_f[0:n, :], op=ALU.pow),
             reads=["rs%d" % ri, "cstf"], writes=["rs2_%d" % ri])
        return rs2, "rs2_%d" % ri

    def norm_T(src, n, dstT, dkey):
        xi_i = rot("xin", 2)
        xi = xin[xi_i]
        xk = "xin%d" % xi_i
        S.dma("sp", xi[0:n, :], src, writes=[xk], key="ldx%d" % xi_i)
        rs2, rk = rstd_of(xi, n, xk, junk=(gen[0], "gen0"))
        S.op("dve", lambda: V.tensor_scalar(out=xnb[0:n, :], in0=xi[0:n, :], scalar1=rs2[0:n, 0:1], scalar2=None,
                                            op0=ALU.mult), reads=[xk, rk], writes=["xnb"])
        for k in range(8):
            S.op("pe", lambda k=k: PE.transpose(out=pT[:, k * 128:k * 128 + n], in_=xnb[0:n, k * 128:(k + 1) * 128],
                                                identity=ident[0:n, 0:n]),
                 reads=["xnb", "ident"], writes=["pT"], inc=(k == 7))
        S.op("dve", lambda: V.tensor_copy(out=dstT, in_=pT[:, :].rearrange("p (k t) -> p k t", k=8)[:, :, 0:n]),
             reads=["pT"], writes=[dkey])

    ppx = [(pp[0], "pp0"), (pp[1], "pp1"), (zbs[0], "zb0"), (zbs[1], "zb1")]

    def next_pp():
        return ppx[rot("ppx", 4)]

    def proj_fm(wc0, xT, n, xkey):
        bank, bkey = next_pp()
        for k in range(8):
            S.op("pe", lambda k=k: PE.matmul(bank[:, 0:n], lhsT=Wbf[:, k, wc0:wc0 + 128], rhs=xT[:, k, 0:n],
                                             start=(k == 0), stop=(k == 7)),
                 reads=["Wbf", xkey], writes=[bkey], inc=(k == 7))
        return bank, bkey

    def proj_tm(wc0, wd, xT, t0, n, xkey):
        bank, bkey = next_pp()
        for k in range(8):
            S.op("pe", lambda k=k: PE.matmul(bank[0:n, 0:wd], lhsT=xT[:, k, t0:t0 + n], rhs=Wbf[:, k, wc0:wc0 + wd],
                                             start=(k == 0), stop=(k == 7)),
                 reads=["Wbf", xkey], writes=[bkey], inc=(k == 7))
        return bank, bkey

    def tm_kv_out(xT, t0, n, xkey, g, vblk, rows_out, outs):
        for wi, which in enumerate(("k", "v")):
            p_, pk_ = proj_tm(wi * 512, 512, xT, t0, n, xkey)
            si = rot("stg", 3)
            sk = "stg%d" % si
            if wi == 0:
                S.op("act", lambda p_=p_, si=si: SC.copy(out=stg[si][0:n, :], in_=p_[0:n, :]), reads=[pk_], writes=[sk])
            else:
                S.op("dve", lambda p_=p_, si=si: V.tensor_copy(out=stg[si][0:n, :], in_=p_[0:n, :]), reads=[pk_], writes=[sk])
            S.dma("sp", outs[(which, g)][rows_out:rows_out + n, :], stg[si][0:n, :], reads=[sk], key="st%d" % si)
            if which == "v" and vblk is not None:
                sv = stg[si][0:n, :].rearrange("t (p two d) -> t p two d", two=2, d=64)
                S.op("pool", lambda sv=sv: G.tensor_copy(out=Vp[0:n, vblk, :, 0:64], in_=sv[:, :, 0, :]),
                     reads=[sk], writes=["Vp"], inc=False)
                S.op("pool", lambda sv=sv: G.tensor_copy(out=Vp[0:n, vblk, :, 128:192], in_=sv[:, :, 1, :]),
                     reads=[sk], writes=["Vp"])

    def logf_out(xT, t0, n, xkey, dst_lf, rows_out, out_ap, dkey="lfa"):
        p_, pk_ = proj_tm(1024, 8, xT, t0, n, xkey)
        S.op("dve", lambda: V.tensor_tensor(out=lft[0:n, :], in0=p_[0:n, 0:8], in1=bft[0:n, :], op=ALU.add),
             reads=[pk_, "bft"], writes=["lft"])
        S.op("act", lambda: SC.activation(out=lft2[0:n, :], in_=lft[0:n, :], func=AF.Exp, scale=-1.0),
             reads=["lft"], writes=["lft2"])
        S.op("act", lambda: SC.activation(out=lft[0:n, :], in_=lft2[0:n, :], func=AF.Ln, bias=1.0),
             reads=["lft2"], writes=["lft"])
        S.op("dve", lambda: V.tensor_scalar(out=dst_lf, in0=lft[0:n, :], scalar1=-1.0, scalar2=None, op0=ALU.mult),
             reads=["lft"], writes=[dkey])
        S.dma("sp", out_ap[rows_out:rows_out + n, :], dst_lf, reads=[dkey], key="stlf")

    def decay_prep(nb, blkspec):
        lf2 = lfa[:, 0:nb, :].rearrange("p b h -> p (b h)")
        S.op("pe", lambda: PE.matmul(pp[0][:, 0:nb * 8], lhsT=tle_f, rhs=lf2, start=True, stop=True),
             reads=["lfa", "cstf"], writes=["pp0"])
        S.op("pe", lambda: PE.matmul(pp[1][:, 0:nb * 8], lhsT=one_f, rhs=lf2, start=True, stop=True),
             reads=["lfa", "cstf"], writes=["pp1"])
        S.op("dve", lambda: V.tensor_copy(out=totS[:, 0:nb, :], in_=pp[1][:, 0:nb * 8].rearrange("p (b h) -> p b h", h=8)),
             reads=["pp1"], writes=["totS"])
        S.op("dve", lambda: V.memset(pre[:, 0, :], 0.0), writes=["pre"])
        for b in range(1, nb + 1):
            S.op("dve", lambda b=b: V.tensor_tensor(out=pre[:, b, :], in0=pre[:, b - 1, :], in1=totS[:, b - 1, :], op=ALU.add),
                 reads=["pre", "totS"], writes=["pre"])
        S.op("dve", lambda: V.tensor_tensor(out=cS[:, 0:nb, :], in0=pp[0][:, 0:nb * 8].rearrange("p (b h) -> p b h", h=8),
                                            in1=pre[:, 0:nb, :], op=ALU.add), reads=["pp0", "pre"], writes=["cS"])
        S.op("dve", lambda: V.tensor_scalar(out=negc[:, 0:nb, :], in0=cS[:, 0:nb, :], scalar1=-1.0, scalar2=None, op0=ALU.mult),
             reads=["cS"], writes=["negc"])
        S.op("dve", lambda: V.tensor_copy(out=comp[0][:, 0:nb, :], in_=cS[:, 0:nb, :]), reads=["cS"], writes=["comp0"])
        S.op("dve", lambda: V.tensor_tensor(out=r1[:, 0:nb, :], in0=cS[:, 0:nb, :], in1=comp[0][:, 0:nb, :], op=ALU.subtract),
             reads=["cS", "comp0"], writes=["r1"])
        S.op("dve", lambda: V.tensor_copy(out=comp[1][:, 0:nb, :], in_=r1[:, 0:nb, :]), reads=["r1"], writes=["comp1"])
        S.op("dve", lambda: V.tensor_tensor(out=cS[:, 0:nb, :], in0=r1[:, 0:nb, :], in1=comp[1][:, 0:nb, :], op=ALU.subtract),
             reads=["r1", "comp1"], writes=["cS"])
        S.op("dve", lambda: V.tensor_copy(out=comp[2][:, 0:nb, :], in_=cS[:, 0:nb, :]), reads=["cS"], writes=["comp2"])
        S.op("pool", lambda: G.memset(CA[:, :, :], 0.0), writes=["CA"])
        S.op("pool", lambda: G.memset(CA[:, :, 0:24], 1.0), writes=["CA"])
        for ci in range(3):
            S.op("dve", lambda ci=ci: V.tensor_scalar(out=CA[:, 0:nb, 32 + 8 * ci:40 + 8 * ci], in0=comp[ci][:, 0:nb, :],
                                                      scalar1=-1.0, scalar2=None, op0=ALU.mult),
                 reads=["comp%d" % ci], writes=["CA"])
        for (blk, nr, c0) in blkspec:
            S.op("pe", lambda blk=blk, nr=nr: PE.transpose(out=pT[0:64, 0:nr], in_=CA[0:nr, blk, :], identity=ident[0:nr, 0:nr]),
                 reads=["CA", "ident"], writes=["pT"])
            S.op("dve", lambda nr=nr, c0=c0: V.tensor_copy(out=Aall[:, c0:c0 + nr], in_=pT[0:64, 0:nr]),
                 reads=["pT"], writes=["Aall"])

    def cb_to_B(nblk, nr, dstB):
        S.op("pool", lambda: G.memset(CB[:, :, 24:64], 0.0), writes=["CB"])
        S.op("pool", lambda: G.memset(CB[:, :, 32:56], 1.0), writes=["CB"])
        for j in range(nblk):
            S.op("pe", lambda j=j: PE.transpose(out=pT[0:64, 0:nr], in_=CB[0:nr, j, :], identity=ident[0:nr, 0:nr]),
                 reads=["CB", "ident"], writes=["pT"])
            S.op("dve", lambda j=j: V.tensor_copy(out=dstB[:, j * nr:(j + 1) * nr], in_=pT[0:64, 0:nr]),
                 reads=["pT"], writes=["Ball"])

    P = {"p1": None, "p2": None}
    noop = lambda: None

    def push(s1, s2, s3, s1b=None, s3a=None, s2c=None):
        t2 = {"s1": s1, "s1b": s1b or noop, "s2": s2, "s2c": s2c or noop, "s3a": s3a or noop, "s3": s3}
        t1, t0 = P["p1"], P["p2"]
        t2["s1"]()
        if t1:
            t1["s2"]()
        if t0:
            t0["s3a"]()
        t2["s1b"]()
        if t0:
            t0["s3"]()
        if t1:
            t1["s2c"]()
        P["p2"], P["p1"] = t1, t2

    def flush():
        t1, t0 = P["p1"], P["p2"]
        if t1:
            t1["s2"]()
        if t0:
            t0["s3a"]()
            t0["s3"]()
        if t1:
            t1["s2c"]()
            t1["s3a"]()
            t1["s3"]()
        P["p1"] = P["p2"] = None

    gbs = [(gb, "gb"), (rb, "rb")]

    def attend_head(g, hh, N, qap, qkey, kt_of, v_of, a_of, bap, blocks, pair_first, pair_last):
        nb = len(blocks)

        def mk_tile(bi, nk, blkid, mask):
            first, last = (bi == 0), (bi == nb - 1)
            zi = rot("zb", 2)
            zb, zk = zbs[zi], "zb%d" % zi
            wi = rot("w", 2)
            w_, wk = wt[wi], "wt%d" % wi
            mms = [(kt_of(blkid, nk), qap, ["KT", qkey], 0, N)]
            if mask is not None:
                if g == 1 and len(mask) > 2:
                    m0 = mask[2] // 2
                    mms.append((ident[0:nk, 0:nk], mask[0][:, m0 * 128:(m0 + 1) * 128], ["ident", mask[1]], m0 * 128, (m0 + 1) * 128))
                else:
                    mms.append((ident[0:nk, 0:nk], mask[0], ["ident", mask[1]], 0, N))
            split = False

            def qk():
                for mi, (l_, r_, rd, c0_, c1_) in enumerate(mms):
                    S.op("pe", lambda l_=l_, r_=r_, mi=mi, c0_=c0_, c1_=c1_: PE.matmul(zb[0:nk, c0_:c1_], lhsT=l_, rhs=r_, start=(mi == 0),
                                                                                       stop=(mi == len(mms) - 1)),
                         reads=rd, writes=[zk], inc=(mi == len(mms) - 1) or (split and mi == 0),
                         selfwait=(split and mi == 1))

            def pv():
                S.op("pe", lambda: PE.matmul(ob[:, 0:N], lhsT=v_of(blkid, nk), rhs=w_[0:nk, 0:N],
                                             start=(pair_first and first), stop=(pair_last and last)),
                     reads=[wk, "Vp", "Vp0"], writes=["ob"])

            if g == 0:
                ei = rot("e", 3)
                e_, ek = et[ei], "et%d" % ei
                si = rot("sp", 2)
                sp_, sk = spt[si], "spt%d" % si
                gi = rot("gb", 2)
                gb_, gk = gbs[gi]

                s1 = qk

                def s1b():
                    S.op("act", lambda: SC.activation(out=e_[0:nk, 0:N], in_=zb[0:nk, 0:N], func=AF.Exp),
                         reads=[zk], writes=[ek])

                def s2():
                    S.op("act", lambda: SC.activation(out=sp_[0:nk, 0:N], in_=e_[0:nk, 0:N], func=AF.Ln, bias=1.0),
                         reads=[ek], writes=[sk])
                    S.op("pe", lambda: PE.matmul(gb_[0:nk, 0:N], lhsT=tri[0:nk, 0:nk], rhs=sp_[0:nk, 0:N], start=True,
                                                 stop=first), reads=[sk, "tri"], writes=[gk], inc=first)
                    if not first:
                        S.op("pe", lambda: PE.matmul(gb_[0:nk, 0:N], lhsT=ones[:, 0:nk], rhs=Ssum[:, 0:N], start=False,
                                                     stop=True), reads=["Ssum", "ones"], writes=[gk])

                def s2c():
                    if not last:
                        if first:
                            S.op("dve", lambda: V.memset(Ssum[:, 0:N], 0.0), writes=["Ssum"])
                            S.op("dve", lambda: V.tensor_copy(out=Ssum[0:nk, 0:N], in_=sp_[0:nk, 0:N]),
                                 reads=[sk, "Ssum"], writes=["Ssum"])
                        else:
                            S.op("dve", lambda: V.tensor_tensor(out=Ssum[0:nk, 0:N], in0=Ssum[0:nk, 0:N],
                                                                 in1=sp_[0:nk, 0:N], op=ALU.add),
                                 reads=[sk, "Ssum"], writes=["Ssum"])

                def s3a():
                    S.op("act", lambda: SC.activation(out=Et[0:nk, 0:N], in_=gb_[0:nk, 0:N], func=AF.Exp, scale=-1.0),
                         reads=[gk], writes=["Et"])
                    S.op("dve", lambda: V.tensor_tensor(out=w_[0:nk, 0:N], in0=e_[0:nk, 0:N], in1=Et[0:nk, 0:N],
                                                        op=ALU.mult), reads=[ek, "Et"], writes=[wk])
                s3 = pv
            else:
                s1 = qk

                def s2():
                    for m in range(4):
                        S.op("act", lambda m=m: SC.activation(out=w_[0:nk, m * 128:(m + 1) * 128], in_=zb[0:nk, m * 128:(m + 1) * 128],
                                                              func=AF.Exp, bias=bap[0][0:nk, m, blkid:blkid + 1]),
                             reads=[zk, bap[1]], writes=[wk], inc=(m == 3))
                    if first:
                        S.op("dve", lambda: V.memset(Pacc[:, 0:N], 0.0), writes=["et0"])
                    S.op("dve", lambda: V.tensor_tensor(out=Pacc[0:nk, 0:N], in0=Pacc[0:nk, 0:N], in1=w_[0:nk, 0:N],
                                                         op=ALU.add), reads=[wk, "et0"], writes=["et0"])
                s3 = pv
                s1b = s3a = s2c = None
            push(s1, s2, s3, s1b, s3a, s2c)

        for bi, (nk, blkid, mask) in enumerate(blocks):
            mk_tile(bi, nk, blkid, mask)
        if g == 1:
            def f2():
                S.op("dve", lambda: V.tensor_copy(out=Pbf[:, 0:N], in_=Pacc[:, 0:N]), reads=["et0"], writes=["spt0"])

            def f3():
                S.op("pe", lambda: PE.matmul(rb[:, 0:N], lhsT=o192[:, 64 * hh:64 * hh + 128], rhs=Pbf[:, 0:N],
                                             start=pair_first, stop=pair_last), reads=["spt0", "o192"], writes=["rb"])
            push(noop, f2, f3)

    def attend_multi(g, s_, blocks, maskap, pre_hook=None):
        N = 256
        nb = len(blocks)
        if g == 1:
            for h in range(8):
                S.op("dve", lambda h=h: V.tensor_scalar(out=Bh[0][:, h * 32:(h + 1) * 32], in0=Ball[:, 0:32],
                                                        scalar1=c2[0:64, 2 + h:3 + h], scalar2=None, op0=ALU.mult),
                     reads=["Ball", "c2"], writes=["Bh0"])

        def mk_tile(bi, nk, blkid, has_mask):
            first, last = (bi == 0), (bi == nb - 1)
            zi = rot("zb", 2)
            zb, zk = zbs[zi], "zb%d" % zi
            wi = rot("w", 2)
            w_, wk = wt[wi], "wt%d" % wi
            extra = []
            if g == 1:
                extra.append((Aall[:, 128 * blkid:128 * blkid + nk], Bh[0][:, 0:N], ["Aall", "Bh0"]))
            if has_mask:
                extra.append((ident[0:nk, 0:nk], maskap[0], ["ident", maskap[1]]))

            if not extra:
                extra.append((zr[0:1, 0:nk], zr[0:1, 0:N], ["zr"]))

            def qk():
                for mi, (l_, r_, rd) in enumerate(extra):
                    S.op("pe", lambda l_=l_, r_=r_, mi=mi: PE.matmul(zb[0:nk, 0:N], lhsT=l_, rhs=r_, start=(mi == 0),
                                                                     stop=False), reads=rd, writes=[zk], inc=False)
                for h in (0, 2, 4, 6, 1, 3, 5, 7):
                    p, hh = h // 2, h % 2
                    rows = slice(64 * hh, 64 * hh + 64)
                    S.op("pe", lambda h=h, p=p, rows=rows: PE.matmul(zb[0:nk, h * 32:(h + 1) * 32],
                                                                     lhsT=KT[rows, p, 128 * blkid:128 * blkid + nk],
                                                                     rhs=QTs[rows, p, 32 * s_:32 * s_ + 32],
                                                                     start=False, stop=(h == 7)),
                         reads=["KTs%d" % blkid, "QTs"], writes=[zk], inc=(h in (6, 7)), selfwait=(h == 1))

            def pv():
                if first:
                    S.op("pe", lambda: PE.matmul(ob[:, 0:128], lhsT=zr[0:1, 0:128], rhs=zr[0:1, 0:128], start=True, stop=False),
                         reads=["zr"], writes=["ob"], inc=False)
                for h in range(8):
                    p, hh = h // 2, h % 2
                    S.op("pe", lambda h=h, p=p, hh=hh: PE.matmul(ob[:, p * 32:(p + 1) * 32],
                                                                 lhsT=Vp[0:nk, blkid, p, 64 * hh:64 * hh + 128],
                                                                 rhs=w_[0:nk, h * 32:(h + 1) * 32],
                                                                 start=False, stop=(last and h == 7)),
                         reads=[wk, "Vps%d" % blkid, "Vp0"], writes=["ob"], inc=(h == 7))

            if g == 0:
                ei = rot("e", 3)
                e_, ek = et[ei], "et%d" % ei
                si = rot("sp", 2)
                sp_, sk = spt[si], "spt%d" % si
                gi = rot("gb", 2)
                gb_, gk = gbs[gi]

                def s1b():
                    S.op("act", lambda: SC.activation(out=e_[0:nk, 0:N], in_=zb[0:nk, 0:N], func=AF.Exp),
                         reads=[zk], writes=[ek])

                def s2():
                    S.op("act", lambda: SC.activation(out=sp_[0:nk, 0:N], in_=e_[0:nk, 0:N], func=AF.Ln, bias=1.0),
                         reads=[ek], writes=[sk])
                    S.op("pe", lambda: PE.matmul(gb_[0:nk, 0:N], lhsT=tri[0:nk, 0:nk], rhs=sp_[0:nk, 0:N], start=True,
                                                 stop=first), reads=[sk, "tri"], writes=[gk], inc=first)
                    if not first:
                        S.op("pe", lambda: PE.matmul(gb_[0:nk, 0:N], lhsT=ones[:, 0:nk], rhs=Ssum[:, 0:N], start=False,
                                                     stop=True), reads=["Ssum", "ones"], writes=[gk])

                def s2c():
                    if not last:
                        if first:
                            S.op("dve", lambda: V.memset(Ssum[:, 0:N], 0.0), writes=["Ssum"])
                            S.op("dve", lambda: V.tensor_copy(out=Ssum[0:nk, 0:N], in_=sp_[0:nk, 0:N]),
                                 reads=[sk, "Ssum"], writes=["Ssum"])
                        else:
                            S.op("dve", lambda: V.tensor_tensor(out=Ssum[0:nk, 0:N], in0=Ssum[0:nk, 0:N],
                                                                 in1=sp_[0:nk, 0:N], op=ALU.add),
                                 reads=[sk, "Ssum"], writes=["Ssum"])

                def s3a():
                    S.op("act", lambda: SC.activation(out=Et[0:nk, 0:N], in_=gb_[0:nk, 0:N], func=AF.Exp, scale=-1.0),
                         reads=[gk], writes=["Et"])
                    S.op("dve", lambda: V.tensor_tensor(out=w_[0:nk, 0:N], in0=e_[0:nk, 0:N], in1=Et[0:nk, 0:N],
                                                        op=ALU.mult), reads=[ek, "Et"], writes=[wk])
                push(qk, s2, pv, s1b, s3a, s2c)
            else:
                def s2():
                    S.op("act", lambda: SC.activation(out=w_[0:nk, 0:N], in_=zb[0:nk, 0:N], func=AF.Exp),
                         reads=[zk], writes=[wk])
                    if first:
                        S.op("dve", lambda: V.memset(Pacc[:, 0:N], 0.0), writes=["et0"])
                    S.op("dve", lambda: V.tensor_tensor(out=Pacc[0:nk, 0:N], in0=Pacc[0:nk, 0:N], in1=w_[0:nk, 0:N],
                                                         op=ALU.add), reads=[wk, "et0"], writes=["et0"])
                push(qk, s2, pv)

        for bi, (nk, blkid, has_mask) in enumerate(blocks):
            if pre_hook is not None:
                pre_hook(bi)
            mk_tile(bi, nk, blkid, has_mask)

        def f2():
            if g == 1:
                S.op("dve", lambda: V.tensor_copy(out=Pbf[:, 0:N], in_=Pacc[:, 0:N]), reads=["et0"], writes=["spt0"])

        def f3():
            if g == 1:
                S.op("pe", lambda: PE.matmul(rb[:, 0:128], lhsT=zr[0:1, 0:128], rhs=zr[0:1, 0:128], start=True, stop=False),
                     reads=["zr"], writes=["rb"], inc=False)
                for h in range(8):
                    p, hh = h // 2, h % 2
                    S.op("pe", lambda h=h, p=p, hh=hh: PE.matmul(rb[:, p * 32:(p + 1) * 32], lhsT=o192[:, 64 * hh:64 * hh + 128],
                                                                 rhs=Pbf[:, h * 32:(h + 1) * 32], start=False, stop=(h == 7)),
                         reads=["spt0", "o192"], writes=["rb"], inc=(h == 7))
            o3 = ob[:, 0:128].rearrange("p (a q) -> p a q", a=4)
            sg3 = SGs[:, :, 32 * s_:32 * s_ + 32]
            dst = smix[:, g * 4:(g + 1) * 4, 32 * s_:32 * s_ + 32]
            if g == 1:
                r3 = rcp[:, 0:128].rearrange("p (a q) -> p a q", a=4)
                S.op("dve", lambda: V.reciprocal(out=rcp[:, 0:128], in_=rb[:, 0:128]), reads=["rb"], writes=["Et"])
                S.op("dve", lambda: V.tensor_tensor(out=rcp[:, 0:128], in0=ob[:, 0:128], in1=rcp[:, 0:128], op=ALU.mult),
                     reads=["ob", "Et"], writes=["Et"])
                S.op("dve", lambda: V.tensor_tensor(out=dst, in0=r3, in1=sg3, op=ALU.mult), reads=["Et", "SGs"], writes=["smix"])
            else:
                S.op("dve", lambda: V.tensor_tensor(out=dst, in0=o3, in1=sg3, op=ALU.mult), reads=["ob", "SGs"], writes=["smix"])
        push(noop, f2, f3)

    def finish_pair(g, N, sgap, sgkey, dst, dkey):
        if g == 1:
            S.op("dve", lambda: V.reciprocal(out=rcp[:, 0:N], in_=rb[:, 0:N]), reads=["rb"], writes=["Et"])
            S.op("dve", lambda: V.tensor_tensor(out=rcp[:, 0:N], in0=ob[:, 0:N], in1=rcp[:, 0:N], op=ALU.mult),
                 reads=["ob", "Et"], writes=["Et"])
            S.op("dve", lambda: V.tensor_tensor(out=dst, in0=rcp[:, 0:N], in1=sgap, op=ALU.mult),
                 reads=["Et", sgkey], writes=[dkey])
        else:
            S.op("dve", lambda: V.tensor_tensor(out=dst, in0=ob[:, 0:N], in1=sgap, op=ALU.mult),
                 reads=["ob", sgkey], writes=[dkey])

    def qg_proj(xT, n, xkey, QTd, SGd, qkey, sgkey):
        for p in range(4):
            p_, pk_ = proj_fm(p * 128, xT, n, xkey)
            S.op("dve", lambda p_=p_, p=p: V.tensor_copy(out=QTd[:, p, 0:n], in_=p_[:, 0:n]), reads=[pk_], writes=[qkey])
        for p in range(4):
            p_, pk_ = proj_fm(512 + p * 128, xT, n, xkey)
            S.op("act", lambda p_=p_, p=p: SC.activation(out=SGd[:, p, 0:n], in_=p_[:, 0:n], func=AF.Silu),
                 reads=[pk_], writes=[sgkey])

    try:
        _stage(0)
        for g in range(2):
            gc = g * 2048
            for r in range(8):
                gi = rot("gen", 2)
                S.dma("sp", gen[gi][:, 0:512], negm[g * 8 + r], writes=["gen%d" % gi], key="ldw%d" % gi)
                S.op("pool", lambda gi=gi, r=r: G.tensor_copy(out=negb[:, r, :], in_=gen[gi][:, 0:512]),
                     reads=["gen%d" % gi], writes=["negb"])
            if g == 1:
                S.op("dve", lambda: V.memset(maskc[:, :, :], 0.0), writes=["maskc"])
                for r in range(8):
                    for m in range(4):
                        if m != r // 2:
                            S.op("dve", lambda r=r, m=m: V.tensor_copy(out=maskc[:, m, r:r + 1], in_=negb[:, r, m * 128:m * 128 + 1]),
                                 reads=["negb", "maskc"], writes=["maskc"])
            _stage(1)
            load_w([(gc + 512, 512), (gc + 1024, 512)] + ([(4096, 8)] if g == 1 else []))
            _stage(2)
            for ch in range(-1, 8):
                nt = 16 if ch < 0 else 512
                nbk = 1 if ch < 0 else 4
                tok0 = 0 if ch < 0 else 16 + 512 * ch
                for j in range(nbk):
                    n = min(128, nt)
                    src = meta if ch < 0 else xall[ch * 512 + j * 128: ch * 512 + (j + 1) * 128, :]
                    norm_T(src, n, xnT[:, :, j * 128:j * 128 + n], "xnT")
                for p in range(4):
                    p_, pk_ = proj_fm(p * 128, xnT, nt, "xnT")
                    S.op("act", lambda p_=p_, p=p, tok0=tok0, nt=nt: SC.activation(out=KT[:, p, tok0:tok0 + nt], in_=p_[:, 0:nt], func=AF.Copy, scale=0.125),
                         reads=[pk_], writes=["KT"])
                for j in range(nbk):
                    n = min(128, nt)
                    vblk = 0 if ch < 0 else 1 + ch * 4 + j
                    tm_kv_out(xnT, j * 128, n, "xnT", g, vblk, tok0 + j * 128, pko)
                    if g == 1:
                        logf_out(xnT, j * 128, n, "xnT", lfa[0:n, vblk, :], tok0 + j * 128, plf)
            if g == 1:
                decay_prep(33, [(0, 16, 0)] + [(1 + j, 128, 16 + 128 * j) for j in range(32)])
                pv_ = pre[:, 2:34, :].rearrange("p (m two) h -> p m two h", two=2)
                S.op("dve", lambda: V.tensor_scalar(out=t1[:, :, :], in0=pv_[:, :, 0, :], scalar1=c2[:, 0:1], scalar2=None,
                                                    op0=ALU.mult), reads=["pre", "c2"], writes=["t1"])
                S.op("dve", lambda: V.scalar_tensor_tensor(out=crefO[:, :, :], in0=pv_[:, :, 1, :], scalar=c2[:, 1:2],
                                                           in1=t1[:, :, :], op0=ALU.mult, op1=ALU.add),
                     reads=["t1", "c2", "pre"], writes=["crefO"])
            _stage(3 + 10 * g)
            norm_T(xs, 64, xnTs[:, :, 0:64], "xnTs")
            for p in range(4):
                p_, pk_ = proj_fm(p * 128, xnTs, 64, "xnTs")
                S.op("act", lambda p_=p_, p=p: SC.activation(out=KTs[:, p, :], in_=p_[:, 0:64], func=AF.Copy, scale=0.125), reads=[pk_], writes=["KTs"])
            for s_ in range(2):
                S.op("pool", lambda s_=s_: G.memset(svt[s_][:, :, :], 0.0), writes=["svt%d" % s_])
                for wi, which in enumerate(("k", "v")):
                    p_, pk_ = proj_tm(wi * 512, 512, xnTs, 32 * s_, 32, "xnTs")
                    si = rot("stg", 3)
                    sk = "stg%d" % si
                    S.op("dve", lambda p_=p_, si=si: V.tensor_copy(out=stg[si][0:32, :], in_=p_[0:32, :]), reads=[pk_], writes=[sk])
                    S.dma("sp", sko[(which, g)][32 * s_:32 * s_ + 32, :], stg[si][0:32, :], reads=[sk], key="st%d" % si)
                    if which == "v":
                        sv = stg[si][0:32, :].rearrange("t (p two d) -> t p two d", two=2, d=64)
                        S.op("pool", lambda sv=sv, s_=s_: G.tensor_copy(out=svt[s_][:, :, 0:64], in_=sv[:, :, 0, :]),
                             reads=[sk], writes=["svt%d" % s_])
                        S.op("pool", lambda sv=sv, s_=s_: G.tensor_copy(out=svt[s_][:, :, 128:192], in_=sv[:, :, 1, :]),
                             reads=[sk], writes=["svt%d" % s_])
                if g == 1:
                    logf_out(xnTs, 32 * s_, 32, "xnTs", slt[s_][:, :], 32 * s_, slf, dkey="slt%d" % s_)
            _stage(4 + 10 * g)
            load_w([(gc, 512), (gc + 1536, 512)])
            qg_proj(xnTs, 64, "xnTs", QTs, SGs, "QTs", "SGs")
            _stage(5 + 10 * g)
            for i in range(4):
                for j in range(4):
                    norm_T(xown[i * 512 + j * 128: i * 512 + (j + 1) * 128, :], 128, xnT[:, :, j * 128:(j + 1) * 128], "xnT")
                qg_proj(xnT, 512, "xnT", QT, SG, "QT", "SG")
                blocks = [(128, 1 + jb, (negb[:, jb - 8 * i, :], "negb", jb - 8 * i) if jb >= 8 * i else None)
                          for jb in range(8 * i + 7, -1, -1)] + [(16, 0, None)]
                for p in range(4):
                    for hh in range(2):
                        h = 2 * p + hh
                        rows = slice(64 * hh, 64 * hh + 64)
                        bap = None
                        if g == 1:
                            bi_ = rot("bh", 2)
                            for m in range(4):
                                S.op("dve", lambda bi_=bi_, h=h, i=i, m=m: V.tensor_scalar(out=biasq[bi_][:, m, :], in0=negc[:, :, h],
                                                                                           scalar1=crefO[:, 4 * i + m, h:h + 1],
                                                                                           scalar2=None, op0=ALU.add),
                                     reads=["negc", "crefO"], writes=["biasq%d" % bi_])
                            S.op("dve", lambda bi_=bi_, i=i: V.tensor_tensor(out=biasq[bi_][:, :, 1 + 8 * i:9 + 8 * i],
                                                                             in0=biasq[bi_][:, :, 1 + 8 * i:9 + 8 * i],
                                                                             in1=maskc[:, :, :], op=ALU.add),
                                 reads=["maskc", "biasq%d" % bi_], writes=["biasq%d" % bi_])
                            bap = (biasq[bi_], "biasq%d" % bi_)

                        def kt_of(blkid, nk, p=p, rows=rows):
                            c0 = 0 if blkid == 0 else 16 + 128 * (blkid - 1)
                            return KT[rows, p, c0:c0 + nk]

                        def v_of(blkid, nk, p=p, hh=hh):
                            return Vp[0:nk, blkid, p, 64 * hh:64 * hh + 128]

                        def a_of(blkid, nk):
                            c0 = 0 if blkid == 0 else 16 + 128 * (blkid - 1)
                            return Aall[:, c0:c0 + nk]

                        _stage(5 + 10 * g + 0.1)
                        attend_head(g, hh, 512, QT[rows, p, :], "QT", kt_of, v_of, a_of, bap, blocks, hh == 0, hh == 1)
                        _stage(5 + 10 * g + 0.2)
                    mi = rot("mix", 2)

                    def fin(mi=mi, p=p, i=i, g=g):
                        finish_pair(g, 512, SG[:, p, :], "SG", mixb[mi][:, :], "mixb%d" % mi)
                        S.dma("sp", mixT[g * 512 + p * 128: g * 512 + (p + 1) * 128, i * 512:(i + 1) * 512], mixb[mi][:, :],
                              reads=["mixb%d" % mi], key="stm%d" % mi)
                    push(noop, noop, fin)
                flush()
            _stage(6 + 10 * g)
            for h in range(8):
                S.op("dve", lambda h=h, g=g: V.tensor_copy(out=negs8[:, h * 32:(h + 1) * 32], in_=negs[0:32, 32 * g:32 * g + 32]),
                     reads=["negs"], writes=["negs8"])
            for s_ in range(2):
                def prep_blk(blk, s_=s_, g=g):
                    kk, vk = "KTs%d" % blk, "Vps%d" % blk
                    if blk == 32:
                        S.op("act", lambda: SC.copy(out=KT[:, :, 4096:4128], in_=KTs[:, :, 32 * s_:32 * s_ + 32]),
                             reads=["KTs"], writes=["KT", kk])
                        S.op("pool", lambda: G.tensor_copy(out=Vp[0:32, 32, :, :], in_=svt[s_][:, :, :]),
                             reads=["svt%d" % s_], writes=["Vp", vk])
                        return
                    si = rot("stg", 3)
                    S.dma("sp", stg[si][:, :], cach[("k", g)][s_, blk * 128:(blk + 1) * 128, :], writes=["stg%d" % si], key="st%d" % si)
                    ci_ = rot("mix", 2)
                    S.op("dve", lambda: V.tensor_copy(out=ckb[ci_][:, :], in_=stg[si][:, :]),
                         reads=["stg%d" % si], writes=["mixb%d" % ci_])
                    for p in range(4):
                        S.op("pe", lambda p=p: PE.transpose(out=pT[:, p * 128:(p + 1) * 128],
                                                            in_=ckb[ci_][:, p * 128:(p + 1) * 128], identity=ident[:, :]),
                             reads=["mixb%d" % ci_, "ident"], writes=["pT"], inc=(p == 3))
                    S.op("act", lambda: SC.activation(out=KT[:, :, blk * 128:(blk + 1) * 128],
                                                      in_=pT[:, 0:512].rearrange("p (q t) -> p q t", q=4), func=AF.Copy, scale=0.125),
                         reads=["pT"], writes=["KT", kk])
                    si2 = rot("stg", 3)
                    S.dma("sp", stg[si2][:, :], cach[("v", g)][s_, blk * 128:(blk + 1) * 128, :], writes=["stg%d" % si2], key="st%d" % si2)
                    sv = stg[si2][:, :].rearrange("t (p two d) -> t p two d", two=2, d=64)
                    S.op("pool", lambda: G.tensor_copy(out=Vp[:, blk, :, 0:64], in_=sv[:, :, 0, :]),
                         reads=["stg%d" % si2], writes=["Vp", vk], inc=False)
                    S.op("pool", lambda: G.tensor_copy(out=Vp[:, blk, :, 128:192], in_=sv[:, :, 1, :]),
                         reads=["stg%d" % si2], writes=["Vp", vk])

                if g == 1:
                    S.dma("sp", lfa[:, 0:32, :], clf[s_].rearrange("(b p) h -> p b h", p=128), reads=[], writes=["lfa"], key="ldlf")
                    S.op("pool", lambda: G.memset(lfa[:, 32, :], 0.0), reads=[], writes=["lfa"])
                    S.op("pool", lambda s_=s_: G.tensor_copy(out=lfa[0:32, 32, :], in_=slt[s_][:, :]),
                         reads=["slt%d" % s_, "lfa"], writes=["lfa"])
                    decay_prep(33, [(j, 128, 128 * j) for j in range(32)] + [(32, 32, 4096)])
                    for ci in range(3):
                        S.op("dve", lambda ci=ci: V.tensor_copy(out=CB[0:32, 0, 8 * ci:8 * ci + 8], in_=comp[ci][0:32, 32, :]),
                             reads=["comp%d" % ci], writes=["CB"])
                    cb_to_B(1, 32, Ball)
                blocks = [(32, 32, True)] + [(128, jb, False) for jb in range(31, -1, -1)]
                order = [blk for (_, blk, _) in blocks]
                emitted = [0]
                LAG = 6

                def pre_hook(bi, order=order, emitted=emitted, prep_blk=prep_blk):
                    tgt = min(len(order), bi + 1 + LAG)
                    while emitted[0] < tgt:
                        prep_blk(order[emitted[0]])
                        emitted[0] += 1
                attend_multi(g, s_, blocks, (negs8[0:32, :], "negs8"), pre_hook)
                flush()
                S.op("pe", lambda: PE.matmul(rb[0:1, 0:1], lhsT=zr[0:1, 0:1], rhs=zr[0:1, 0:1], start=True, stop=True),
                     reads=["KTs%d" % b_ for b_ in range(33)] + ["Vps%d" % b_ for b_ in range(33)] + ["zr"],
                     writes=["KT", "Vp", "rb"])

        _stage(30)
        load_w([(0, 1024)], src=w_out, fold=False)
        mview = mixT.rearrange("(k p) t -> p k t", p=128)

        def out_block(mt_ap, mkey, xsrc, n, ydst):
            xi_i = rot("xin", 2)
            xi, xk = xin[xi_i], "xin%d" % xi_i
            S.dma("sp", xi[0:n, :], xsrc, writes=[xk], key="ldx%d" % xi_i)
            for half in range(2):
                bank, bkey = next_pp()
                for k in range(8):
                    S.op("pe", lambda k=k, bank=bank, half=half: PE.matmul(bank[0:n, :], lhsT=mt_ap[:, k, 0:n],
                                                                           rhs=Wbf[:, k, half * 512:(half + 1) * 512],
                                                                           start=(k == 0), stop=(k == 7)),
                         reads=["Wbf", mkey], writes=[bkey], inc=(k == 7))
                S.op("dve", lambda bank=bank, half=half: V.tensor_tensor(out=xi[0:n, half * 512:(half + 1) * 512], in0=bank[0:n, :],
                                                                         in1=xi[0:n, half * 512:(half + 1) * 512], op=ALU.add),
                     reads=[bkey, xk], writes=[xk])
            rs2, rk = rstd_of(xi, n, xk)
            gi = rot("gen", 2)
            S.op("dve", lambda gi=gi: V.scalar_tensor_tensor(out=gen[gi][0:n, 0:1024], in0=xi[0:n, :], scalar=rs2[0:n, 0:1],
                                                             in1=fgt[0:n, :], op0=ALU.mult, op1=ALU.mult),
                 reads=[xk, rk, "fgt"], writes=["gen%d" % gi])
            S.dma("sp", ydst, gen[gi][0:n, 0:1024], reads=["gen%d" % gi], key="ldw%d" % gi)

        for blk in range(16):
            mi = rot("mT", 2)
            mt_ = xnT[:, :, mi * 128:(mi + 1) * 128]
            mk_ = "xnTm%d" % mi
            S.dma("sp", mt_, mview[:, :, blk * 128:(blk + 1) * 128], writes=([mk_, "xnT"] if blk < 2 else [mk_]), key="ldm%d" % mi)
            out_block(mt_, mk_, xown[blk * 128:(blk + 1) * 128, :], 128, y_own[blk * 128:(blk + 1) * 128, :])
        out_block(smix, "smix", xs, 64, ys)


    except _Stop:
        pass
    S.emit()
    es.close()
    return nc


def _consts():
    c = np.zeros((128, CW), np.float32)
    j = np.arange(128)
    c[:, C_ID:C_ID + 128] = np.eye(128)
    c[:, C_TRI:C_TRI + 128] = (j[:, None] >= j[None, :])
    c[:, C_ONE:C_ONE + 128] = 1.0
    c[:, C_TLE:C_TLE + 128] = (j[:, None] <= j[None, :])
    c[:, C_O192:C_O192 + 64] = 1.0
    c[:, C_O192 + 128:C_O192 + 192] = 1.0
    k = np.arange(32)
    c[0:32, C_NS:C_NS + 32] = np.where(k[:, None] < k[None, :], 0.0, NEGV)
    c[0:32, C_NS + 32:C_NS + 64] = np.where(k[:, None] <= k[None, :], 0.0, NEGV)
    c[:, C_NH] = -0.5
    return c


def _masks(par):
    m = np.zeros((16, 128, 512), np.float32)
    s = np.arange(128)[:, None]
    for r in range(8):
        for mm in range(4):
            kpos = r * 128 + s
            qpos = (par + 2 * mm) * 128 + np.arange(128)[None, :]
            m[r, :, mm * 128:(mm + 1) * 128] = np.where(kpos < qpos, 0.0, NEGV)
            m[8 + r, :, mm * 128:(mm + 1) * 128] = np.where(kpos <= qpos, 0.0, NEGV)
    return m


_NC = None
_SIM_HOOK = None


def kernel(x_prompt, x_sample, cache_a_k, cache_a_v, cache_b_k, cache_b_v, cache_b_logf,
           meta_tokens, norm_g, w_in, b_f, w_out, final_g):
    global _NC
    f = lambda a: np.ascontiguousarray(np.asarray(a, dtype=np.float32))
    x_prompt, x_sample = f(x_prompt), f(x_sample)
    cak, cav, cbk, cbv, clf = f(cache_a_k), f(cache_a_v), f(cache_b_k), f(cache_b_v), f(cache_b_logf)
    meta_tokens, norm_g, w_in, b_f, w_out, final_g = f(meta_tokens), f(norm_g), f(w_in), f(b_f), f(w_out), f(final_g)
    if _NC is None:
        _NC = build()
    nc = _NC
    cst = _consts()
    in_maps = []
    for core in range(8):
        b, c = core // 2, core % 2
        c2 = np.zeros((128, 16), np.float32)
        c2[:, 0] = 1.0 if c == 0 else 0.0
        c2[:, 1] = 1.0 if c == 1 else 0.0
        for h in range(8):
            for cc in range(3):
                c2[cc * 8 + h, 2 + h] = 1.0
                c2[32 + cc * 8 + h, 2 + h] = 1.0
        sl = slice(2 * core, 2 * core + 2)
        in_maps.append({
            "xall": x_prompt[b],
            "meta": meta_tokens,
            "xown": f(x_prompt[b].reshape(16, 2, 128, 1024)[:, c].reshape(2048, 1024)),
            "xs": f(x_sample[sl].reshape(64, 1024)),
            "w_in": w_in[0], "w_out": w_out[0],
            "ng": f(norm_g[0].reshape(8, 128).T),
            "fg": f(np.broadcast_to(final_g[None, :], (128, 1024))),
            "bfb": f(np.broadcast_to(b_f[0][None, :], (128, 8))),
            "cak": f(cak[0, sl].reshape(2, 4096, 512)), "cav": f(cav[0, sl].reshape(2, 4096, 512)),
            "cbk": f(cbk[0, sl].reshape(2, 4096, 512)), "cbv": f(cbv[0, sl].reshape(2, 4096, 512)),
            "clf": f(clf[0, sl]),
            "cst": cst, "cst2": c2, "negm": _masks(c),
        })
    if _SIM_HOOK is not None:
        return _SIM_HOOK(nc, in_maps)
    res = run_bass_kernel_spmd(nc, in_maps, core_ids=list(range(8)))
    return _assemble(res.results)


def _assemble(R):
    y_prompt = np.zeros((4, 4096, 1024), np.float32)
    y_sample = np.zeros((16, 32, 1024), np.float32)
    pk = {k: np.zeros((1, 4, 4112, 8, 64), np.float32) for k in ("pka", "pva", "pkb", "pvb")}
    p_lf = np.zeros((1, 4, 4112, 8), np.float32)
    sk = {k: np.zeros((1, 16, 32, 8, 64), np.float32) for k in ("sak", "sav", "sbk", "sbv")}
    s_lf = np.zeros((1, 16, 32, 8), np.float32)
    for core in range(8):
        b, c = core // 2, core % 2
        r = R[core]
        y_prompt[b].reshape(16, 2, 128, 1024)[:, c] = np.asarray(r["y_own"]).reshape(16, 128, 1024)
        y_sample[2 * core:2 * core + 2] = np.asarray(r["ys"]).reshape(2, 32, 1024)
        if c == 0:
            for k in pk:
                pk[k][0, b] = np.asarray(r[k]).reshape(4112, 8, 64)
            p_lf[0, b] = np.asarray(r["plf"])
        for k in sk:
            sk[k][0, 2 * core:2 * core + 2] = np.asarray(r[k]).reshape(2, 32, 8, 64)
        s_lf[0, 2 * core:2 * core + 2] = np.asarray(r["slf"]).reshape(2, 32, 8)
    return (y_prompt, y_sample, pk["pka"], pk["pva"], pk["pkb"], pk["pvb"], p_lf,
            sk["sak"], sk["sav"], sk["sbk"], sk["sbv"], s_lf)
```
